# Optimizing a Trainium2 kernel written in Bass

```python
import math
import jax, jax.numpy as jnp
from jax import lax
import numpy as np

D_MODEL = 2048
BATCH = 2
SEQ = 4096
DEPTH = 4

N_META = 16
BLOCK = 128
WINDOW = 128
HEAD_DIM = 64
A_HEADS = 16
A_KV_HEADS = 2
A_WIDTH = 1024
A_KV_WIDTH = 128
B_HEADS = 16
B_WIDTH = 1024
B_Q_RANK = 512
B_KV_RANK = 256
IDX_HEADS = 8
IDX_DIM = 64
TOPK_MAX = 256
N_BUCKETS = 32
MAX_DISTANCE = 128
EPS = 1e-6
NEG = -1e30
INVALID_POS = 1 << 30

kernel_name = "hybrid_swa_sink_dsa_gated_trunk"

IN_SPLITS = (A_WIDTH, A_KV_WIDTH, A_KV_WIDTH, A_WIDTH,
             B_Q_RANK, B_KV_RANK, B_WIDTH, IDX_DIM, IDX_HEADS,
             D_MODEL, D_MODEL)
IN_COLS = 8264


def _split_points():
    return [int(v) for v in np.cumsum(np.array(IN_SPLITS))[:-1]]


def _rmsnorm(x, g):
    xf = x.astype(jnp.float32)
    y = xf * lax.rsqrt(jnp.mean(xf * xf, axis=-1, keepdims=True) + EPS)
    return (y * g.astype(jnp.float32)).astype(x.dtype)


def _t5_bucket(dist):
    n = jnp.maximum(dist, 0)
    max_exact = N_BUCKETS // 2
    nf = jnp.maximum(n, 1).astype(jnp.float32)
    large = max_exact + (jnp.log(nf / max_exact) / math.log(MAX_DISTANCE / max_exact)
                         * (N_BUCKETS - max_exact)).astype(jnp.int32)
    large = jnp.minimum(large, N_BUCKETS - 1)
    return jnp.where(n < max_exact, n, large)


def _sink_attention(q, k, v, qpos, kpos, sinks, bias_a):
    G, R = A_KV_HEADS, A_HEADS // A_KV_HEADS
    s = jnp.einsum('...qgrd,...kgd->...grqk', q.astype(jnp.float32),
                   k.astype(jnp.float32)) * (HEAD_DIM ** -0.5)
    d = qpos[..., :, None] - kpos[..., None, :]
    mask = (d >= 0) & ((d < WINDOW) | (kpos[..., None, :] < N_META))
    b = jnp.moveaxis(bias_a[_t5_bucket(d)], -1, -3)
    b = b.reshape(d.shape[:-2] + (G, R) + d.shape[-2:]).astype(jnp.float32)
    logits = jnp.where(mask[..., None, None, :, :], s + b, NEG)
    sk = sinks.reshape(G, R)[:, :, None, None].astype(jnp.float32)
    m = jnp.maximum(jnp.max(logits, axis=-1, keepdims=True), sk)
    p = jnp.exp(logits - m)
    p = p / (jnp.sum(p, axis=-1, keepdims=True) + jnp.exp(sk - m))
    o = jnp.einsum('...grqk,...kgd->...qgrd', p, v.astype(jnp.float32))
    return o.astype(q.dtype)


def _mixer_a(q, k, v, sinks, bias_a):
    Bn, L = q.shape[0], q.shape[1]
    S = L - N_META
    nb = S // BLOCK
    G, R = A_KV_HEADS, A_HEADS // A_KV_HEADS
    q = q.reshape(Bn, L, G, R, HEAD_DIM)
    qm, qr = q[:, :N_META], q[:, N_META:]
    km, kr = k[:, :N_META], k[:, N_META:]
    vm, vr = v[:, :N_META], v[:, N_META:]
    meta_pos = jnp.arange(N_META, dtype=jnp.int32)
    o_meta = _sink_attention(qm, km, vm, meta_pos, meta_pos, sinks, bias_a)
    qb = qr.reshape(Bn, nb, BLOCK, G, R, HEAD_DIM)
    kb = kr.reshape(Bn, nb, BLOCK, G, HEAD_DIM)
    vb = vr.reshape(Bn, nb, BLOCK, G, HEAD_DIM)
    pad = ((0, 0), (1, 0), (0, 0), (0, 0), (0, 0))
    kprev = jnp.pad(kb, pad)[:, :-1]
    vprev = jnp.pad(vb, pad)[:, :-1]
    kmb = jnp.broadcast_to(km[:, None], (Bn, nb, N_META, G, HEAD_DIM))
    vmb = jnp.broadcast_to(vm[:, None], (Bn, nb, N_META, G, HEAD_DIM))
    kcat = jnp.concatenate([kmb, kprev, kb], axis=2)
    vcat = jnp.concatenate([vmb, vprev, vb], axis=2)
    blk_pos = N_META + jnp.arange(S, dtype=jnp.int32).reshape(nb, BLOCK)
    prev_pos = jnp.concatenate([jnp.full((1, BLOCK), INVALID_POS, jnp.int32), blk_pos[:-1]], axis=0)
    kpos = jnp.concatenate([jnp.broadcast_to(meta_pos, (nb, N_META)), prev_pos, blk_pos], axis=1)
    o_real = _sink_attention(qb, kcat, vcat, blk_pos, kpos, sinks, bias_a)
    o_real = o_real.reshape(Bn, S, A_WIDTH)
    return jnp.concatenate([o_meta.reshape(Bn, N_META, A_WIDTH), o_real], axis=1)


def _mixer_b(q_lat, ckv, q_idx, k_idx, w_idx, bias_b, w_uv, topk):
    Bn, L = q_lat.shape[0], q_lat.shape[1]
    S = L - N_META
    nb = S // BLOCK
    kpos = jnp.arange(L, dtype=jnp.int32)
    k_idx32 = k_idx.astype(jnp.float32)

    def attend(ql, qi, wi, qpos):
        sc = jnp.einsum('bqhd,bkd->bqhk', qi.astype(jnp.float32), k_idx32) * (IDX_DIM ** -0.5)
        isc = jnp.einsum('bqh,bqhk->bqk', wi.astype(jnp.float32), jax.nn.relu(sc))
        isc = jnp.where(kpos[None, :] <= qpos[:, None], isc, NEG)
        _, idx = lax.top_k(isc, topk)
        valid = idx <= qpos[None, :, None]
        sel = jax.vmap(lambda c, i: c[i])(ckv, idx).astype(jnp.float32)
        s = jnp.einsum('bqhr,bqkr->bqhk', ql.astype(jnp.float32), sel) * (HEAD_DIM ** -0.5)
        b = bias_b[_t5_bucket(qpos[None, :, None] - idx)].astype(jnp.float32)
        s = jnp.where(valid[:, :, None, :], s + jnp.moveaxis(b, -1, -2), NEG)
        p = jax.nn.softmax(s, axis=-1)
        return jnp.einsum('bqhk,bqkr->bqhr', p, sel).astype(ql.dtype)

    meta_pos = jnp.arange(N_META, dtype=jnp.int32)
    o_meta = attend(q_lat[:, :N_META], q_idx[:, :N_META], w_idx[:, :N_META], meta_pos)

    def to_blocks(a):
        a = a[:, N_META:]
        a = a.reshape((Bn, nb, BLOCK) + a.shape[2:])
        return jnp.moveaxis(a, 1, 0)

    blk_pos = N_META + jnp.arange(S, dtype=jnp.int32).reshape(nb, BLOCK)
    xs = (to_blocks(q_lat), to_blocks(q_idx), to_blocks(w_idx), blk_pos)
    o_real = lax.map(lambda a: attend(a[0], a[1], a[2], a[3]), xs)
    o_real = jnp.moveaxis(o_real, 0, 1).reshape(Bn, S, B_HEADS, B_KV_RANK)
    o_lat = jnp.concatenate([o_meta, o_real], axis=1)
    o = jnp.einsum('blhr,hrd->blhd', o_lat, w_uv)
    return o.reshape(Bn, L, B_WIDTH)


def setup_inputs(seed: int = 0) -> dict:
    key = jax.random.key(seed)
    ks = jax.random.split(key, 20)
    f = jnp.float32
    nrm = lambda k, shape, scale: jax.random.normal(k, shape, f) * scale
    return {
        "x": nrm(ks[0], (BATCH, SEQ, D_MODEL), 1.0),
        "meta_tokens": nrm(ks[1], (N_META, D_MODEL), 1.0),
        "bias_table": nrm(ks[2], (N_BUCKETS, A_HEADS + B_HEADS), 0.5),
        "norm_g": 1.0 + nrm(ks[3], (DEPTH, D_MODEL), 0.02),
        "w_in": nrm(ks[4], (DEPTH, D_MODEL, IN_COLS), D_MODEL ** -0.5),
        "q_norm_g": 1.0 + nrm(ks[5], (DEPTH, B_Q_RANK), 0.02),
        "kv_norm_g": 1.0 + nrm(ks[6], (DEPTH, B_KV_RANK), 0.02),
        "w_qb": nrm(ks[7], (DEPTH, B_Q_RANK, B_WIDTH), B_Q_RANK ** -0.5),
        "w_iq": nrm(ks[8], (DEPTH, B_Q_RANK, IDX_HEADS * IDX_DIM), B_Q_RANK ** -0.5),
        "w_uk": nrm(ks[9], (DEPTH, B_HEADS, HEAD_DIM, B_KV_RANK), B_KV_RANK ** -0.5),
        "w_uv": nrm(ks[10], (DEPTH, B_HEADS, B_KV_RANK, HEAD_DIM), B_KV_RANK ** -0.5),
        "sinks": nrm(ks[11], (DEPTH, A_HEADS), 1.0),
        "w_proj_a": nrm(ks[12], (DEPTH, A_WIDTH, D_MODEL), A_WIDTH ** -0.5),
        "w_proj_b": nrm(ks[13], (DEPTH, B_WIDTH, D_MODEL), B_WIDTH ** -0.5),
        "w_out": nrm(ks[14], (DEPTH, D_MODEL, D_MODEL), D_MODEL ** -0.5),
        "final_g": 1.0 + nrm(ks[15], (D_MODEL,), 0.02),
    }


def reference(x, meta_tokens, bias_table, norm_g, w_in, q_norm_g, kv_norm_g, w_qb, w_iq,
              w_uk, w_uv, sinks, w_proj_a, w_proj_b, w_out, final_g):
    Bn, S, D = x.shape
    L = S + N_META
    topk = min(TOPK_MAX, S // 4)
    h = jnp.concatenate([jnp.broadcast_to(meta_tokens.astype(x.dtype)[None], (Bn, N_META, D)), x], axis=1)
    bias_a = bias_table[:, :A_HEADS]
    bias_b = bias_table[:, A_HEADS:]
    splits = _split_points()
    for l in range(DEPTH):
        u = _rmsnorm(h, norm_g[l])
        proj = jnp.einsum('bld,dc->blc', u, w_in[l])
        qa, ka, va, za, cq, ckv, zb, kidx, widx, ga, gb = jnp.split(proj, splits, axis=-1)
        ya = _mixer_a(qa.reshape(Bn, L, A_HEADS, HEAD_DIM),
                      ka.reshape(Bn, L, A_KV_HEADS, HEAD_DIM),
                      va.reshape(Bn, L, A_KV_HEADS, HEAD_DIM), sinks[l], bias_a)
        ya = ya * jax.nn.silu(za)
        cq = _rmsnorm(cq, q_norm_g[l])
        ckv = _rmsnorm(ckv, kv_norm_g[l])
        qb = jnp.einsum('blr,rc->blc', cq, w_qb[l]).reshape(Bn, L, B_HEADS, HEAD_DIM)
        q_lat = jnp.einsum('blhd,hdr->blhr', qb, w_uk[l])
        q_idx = jnp.einsum('blr,rc->blc', cq, w_iq[l]).reshape(Bn, L, IDX_HEADS, IDX_DIM)
        yb = _mixer_b(q_lat, ckv, q_idx, kidx, widx * (IDX_HEADS ** -0.5), bias_b, w_uv[l], topk)
        yb = yb * jax.nn.silu(zb)
        merged = (jax.nn.sigmoid(ga) * jnp.einsum('blc,cd->bld', ya, w_proj_a[l])
                  + jax.nn.sigmoid(gb) * jnp.einsum('blc,cd->bld', yb, w_proj_b[l]))
        h = h + jnp.einsum('bld,de->ble', merged, w_out[l])
    return _rmsnorm(h, final_g)[:, N_META:]
```

```python
import math
from contextlib import ExitStack
import numpy as np
import ml_dtypes
import concourse.bass as bass
import concourse.mybir as mybir
from concourse.bass_utils import run_bass_kernel_spmd

F32, BF16 = mybir.dt.float32, mybir.dt.bfloat16
ALU = mybir.AluOpType
AF = mybir.ActivationFunctionType
AXX = mybir.AxisListType.X
NPBF = ml_dtypes.bfloat16

D = 2048
SEQ = 4096
NM = 16
L = SEQ + NM
NT = 4112
DEPTH = 4
NB = 32
TOPK = 256
NEGB = -30000.0
O_QA, O_KA, O_VA, O_ZA, O_CQ, O_CKV, O_ZB, O_KI, O_WI, O_GA, O_GB = (
    0, 1024, 1152, 1280, 2304, 2816, 3072, 4096, 4160, 4168, 6216)
GROUPS = [(0, 272, [(32, 0, 16), (0, 16, 128), (1, 144, 128)])] + [(16 + 256 * g, 256, [(2 * g, 0, 128), (2 * g + 1, 128, 128)]) for g in range(1, 16)]
NLAYER_BUILD = DEPTH
C_QA, C_ZA, C_ZB, C_QB, C_QI, C_CQN = 0, 8, 16, 24, 32, 36
NPJ = 40
FB_A0, FB_A1, FB_B0, FB_B1, FB_MA, FB_MB, FB_QA, FB_QB = 0, 255, 510, 765, 1020, 1163, 1306, 1337
NF = 1368
NBIS = 16


class _Eng:
    def __init__(self, name, h, sem):
        self.name, self.h, self.sem, self.count, self.waited = name, h, sem, 0, {}


class Sync:
    def __init__(self, nc, st):
        self.nc, self.st = nc, st
        self.engs = {}
        for name, h in (("pe", nc.tensor), ("act", nc.scalar), ("dve", nc.vector),
                        ("pool", nc.gpsimd), ("sp", nc.sync)):
            self.engs[name] = _Eng(name, h, st.enter_context(nc.semaphore("sem_" + name)))
        self.last_w, self.readers, self.dsems = {}, {}, {}
        self.dead = False
        self.nrot = 0

    def _dsem(self, key):
        if key not in self.dsems:
            self.dsems[key] = [self.st.enter_context(self.nc.semaphore("ds%d" % len(self.dsems))), 0]
        return self.dsems[key]

    def _deps(self, reads, writes):
        deps = []
        for r in reads:
            t = self.last_w.get(r)
            if t is not None:
                deps.append(t)
        for w in writes:
            t = self.last_w.get(w)
            if t is not None:
                deps.append(t)
            deps.extend(self.readers.get(w, ()))
        return deps

    def _wait(self, e, deps, is_dma=False):
        best = {}
        for (sem, val, src) in deps:
            if (not is_dma) and e.name == "pe" and src == "pe":
                continue
            k = id(sem)
            if k not in best or best[k][1] < val:
                best[k] = (sem, val)
        for k, (sem, val) in best.items():
            if e.waited.get(k, 0) < val:
                e.h.wait_ge(sem, val)
                e.waited[k] = val

    def _commit(self, tok, reads, writes):
        for r in reads:
            self.readers.setdefault(r, []).append(tok)
        for w in writes:
            self.last_w[w] = tok
            self.readers[w] = []

    def op(self, eng, fn, reads=(), writes=()):
        if self.dead:
            return None
        e = self.engs[eng]
        self._wait(e, self._deps(reads, writes))
        if e.count >= 30000:
            e.sem = self.st.enter_context(self.nc.semaphore("sem_%s_%d" % (e.name, len(self.dsems) + self.nrot)))
            self.nrot += 1
            e.count = 0
        ins = fn(e.h)
        e.count += 1
        ins.then_inc(e.sem, 1)
        self._commit((e.sem, e.count, eng), reads, writes)
        return ins

    def dma(self, eng, out, in_, reads=(), writes=(), key=None, **kw):
        if self.dead:
            return None
        e = self.engs[eng]
        self._wait(e, self._deps(reads, writes), is_dma=True)
        ds = self._dsem(key if key is not None else writes[0])
        ins = e.h.dma_start(out=out, in_=in_, **kw)
        ds[1] += 16
        ins.then_inc(ds[0], 16)
        self._commit((ds[0], ds[1], "dma"), reads, writes)
        return ins

    def finish(self, eng="sp"):
        deps = list(self.last_w.values())
        for v in self.readers.values():
            deps.extend(v)
        self._wait(self.engs[eng], deps, is_dma=True)


def _bucket(d):
    d = np.maximum(d, 0)
    nf = np.maximum(d, 1).astype(np.float32)
    large = 16 + (np.log(nf / np.float32(16)) / np.float32(math.log(128 / 16)) * np.float32(16)).astype(np.int32)
    large = np.minimum(large, 31)
    return np.where(d < 16, d, large)


def _oh_cols(dvals, allowed):
    n = len(dvals)
    oh = np.zeros((33, n), np.float32)
    b = _bucket(np.asarray(dvals))
    for i in range(n):
        if allowed[i]:
            oh[b[i], i] += 1.0
            oh[31, i] -= 1.0
        else:
            oh[32, i] = 1.0
    return oh


def _tables():
    j = np.arange(-127, 128)
    cols = []
    cols.append(_oh_cols(j, (j >= 0) & (j < 128)))
    cols.append(_oh_cols(128 + j, (128 + j >= 0) & (128 + j < 128)))
    cols.append(_oh_cols(j, j >= 0))
    cols.append(_oh_cols(128 + j, 128 + j >= 0))
    jm_ = np.arange(-15, 128)
    dm = 16 + jm_
    cols.append(_oh_cols(dm, np.ones_like(dm, bool)))
    cols.append(_oh_cols(dm, np.ones_like(dm, bool)))
    jq = np.arange(-15, 16)
    cols.append(_oh_cols(jq, jq >= 0))
    cols.append(_oh_cols(jq, jq >= 0))
    oh = np.concatenate(cols, 1)
    assert oh.shape[1] == NF
    qq = np.arange(128)[:, None]
    kk = np.arange(128)[None, :]
    cm = np.where(qq - kk >= 0, 0.0, -1e30).astype(np.float32)
    kq = np.zeros((128, 33), np.float32)
    for s in range(32):
        pos = 16 + 128 * s + np.arange(128)
        kq[:, s] = np.minimum(TOPK, pos + 1)
    kq[:, 32] = 1.0
    kq[:16, 32] = np.arange(16) + 1
    cmq = np.zeros((128, 16), np.float32)
    cmq[:16] = np.where(np.arange(16)[:, None] >= np.arange(16)[None, :], 0.0, -1e30)
    eye = np.eye(128, dtype=np.float32)
    anti = np.ascontiguousarray(eye[::-1])
    jm = np.zeros((128, 256), np.float32)
    jm[:, 0:128] = anti
    for i in range(16):
        jm[i, 128 + 15 - i] = 1.0
    jm[16, 128 + 16] = 1.0
    bis = np.tile((0.5 ** (np.arange(NBIS) + 1)).astype(np.float32)[None, :], (128, 1))
    return dict(oh=oh, cmask=cm, kq=kq, cmq=cmq, ident=eye, bis=bis, jm=jm)


def _pc(v, n):
    return np.ascontiguousarray(np.asarray(v, np.float32).reshape(n, 128).T)


def build(nlayer=DEPTH, dbg=False, stop=None, groups=None):
    nc = bass.Bass("TRN2", target_bir_lowering=False)
    groups = GROUPS if groups is None else groups

    def din(name, shape, dt=F32):
        return nc.dram_tensor(name, list(shape), dt, kind="ExternalInput").ap()

    def dout(name, shape, dt=F32):
        return nc.dram_tensor(name, list(shape), dt, kind="ExternalOutput").ap()

    hT = din("hT", [D, NT])
    ident_d = din("ident", [128, 128])
    g_all = din("g_all", [DEPTH + 1, 128, 16])
    w_in_d = din("w_in", [DEPTH, D, 8264])
    qg_d = din("qg", [DEPTH, 128, 4])
    kvg_d = din("kvg", [DEPTH, 128, 2])
    w_qb_d = din("w_qb", [DEPTH, 512, 1024])
    w_iq_d = din("w_iq", [DEPTH, 512, 512])
    w_uk_d = din("w_uk", [DEPTH, 16, 64, 256])
    w_uv_d = din("w_uv", [DEPTH, 16, 256, 64])
    sinks_d = din("sinks", [DEPTH, 16])
    w_pa_d = din("w_pa", [DEPTH, 1024, D])
    w_pb_d = din("w_pb", [DEPTH, 1024, D])
    w_out_d = din("w_out", [DEPTH, D, D])
    btab_d = din("btab", [32, 32])
    cmask_d = din("cmask", [128, 128])
    kq_d = din("kq", [128, 33])
    cmq_d = din("cmq", [128, 16])
    oh_d = din("oh", [33, NF])
    bis_d = din("bis", [128, NBIS])
    jm_d = din("jm", [128, 256])
    final_out = dout("out", [D, SEQ])
    fscr = nc.dram_tensor("fscr", [32, NF], BF16)
    sscr = nc.dram_tensor("sscr", [16, 16, 128], BF16)
    hbuf = [nc.dram_tensor("hbuf%d" % i, [D, NT], F32).ap() for i in range(2)]
    ksA = nc.dram_tensor("ksA", [128, NT], BF16).ap()
    ksV = nc.dram_tensor("ksV", [NT, 128], BF16).ap()
    dbg_out = dout("dbg", [128, 4112]) if dbg else None

    st = ExitStack()
    with st:
        S = Sync(nc, st)

        def sb(name, shape, dt):
            return st.enter_context(nc.sbuf_tensor("s_" + name, list(shape), dt))

        def ps(name, shape=(128, 512), dt=F32):
            return st.enter_context(nc.psum_tensor(name, list(shape), dt))

        hrot = sb("hrot", [128, 3, 272], F32)
        u = sb("u", [128, 16, 272], BF16)
        rstd = sb("rstd", [128, 272], F32)
        sq = sb("sq", [128, 2, 272], F32)
        wbuf = sb("wbuf", [128, 3, 4096], BF16)
        identf = sb("identf", [128, 128], F32)
        identb = sb("identb", [128, 128], BF16)
        onesf = sb("onesf", [128, 128], F32)
        onesb = sb("onesb", [128, 128], BF16)
        gall = sb("gall", [128, DEPTH + 1, 16], F32)
        kvg = sb("kvg", [128, DEPTH, 2], F32)
        qg = sb("qg", [128, DEPTH, 4], F32)
        kso = sb("kso", [128, 272], BF16)
        kso_tm = sb("kso_tm", [128, 128], BF16)
        ckv32 = sb("ckv32", [128, 2, 272], F32)
        vfm = sb("vfm", [128, 272], BF16)
        pj = sb("pj", [128, NPJ, 272], BF16)
        cq32 = sb("cq32", [128, 4, 272], F32)
        kfm = sb("kfm", [128, 3, L], BF16)
        ktm = sb("ktm", [128, 33, 256], BF16)
        swk = sb("swk", [128, 2, 2, 272], BF16)
        swv = sb("swv", [128, 2, 3, 256], BF16)
        isc = sb("isc", [128, L], F32)
        mneg = sb("mneg", [128, L], BF16)
        TA = sb("TA", [128, 2, 16, 128], BF16)
        TB = sb("TB", [128, 2, 16, 128], BF16)
        TmA0 = sb("TmA0", [128, 16, 128], BF16)
        TmB0 = sb("TmB0", [128, 16, 128], BF16)
        TmZ = sb("TmZ", [128, 16, 128], BF16)
        TqA = sb("TqA", [128, 16, 16], BF16)
        TqB = sb("TqB", [128, 16, 16], BF16)
        I4 = sb("I4", [128, 4, 128], BF16)
        jmb = sb("jmb", [128, 256], BF16)
        qlat = sb("qlat", [128, 2, 4, 128], BF16)
        olat = sb("olat", [128, 2, 4, 128], BF16)
        PT = sb("PT", [128, 3, 512], BF16)
        rtmp = sb("rtmp", [128, 2, 512], F32)
        rden = sb("rden", [128, 512], F32)
        wuk = sb("wuk", [128, 8, 256], BF16)
        wuv = sb("wuv", [128, 2, 16, 64], BF16)
        wq = sb("wq", [128, 3, 8], F32)
        kq = sb("kq", [128, 33], F32)
        cmask = sb("cmask", [128, 128], F32)
        cmq = sb("cmq", [128, 16], F32)
        bis = sb("bis", [128, NBIS], F32)
        btab = sb("btab", [128, 32], F32)
        fsb = sb("fsb", [128, NF], BF16)
        sk = sb("sk", [128, 16], F32)
        bsm = sb("bsm", [128, 16], F32)
        wh = sb("wh", [128, NBIS], F32)
        gsig = sb("gsig", [128, 2, 272], F32)
        mtmp = sb("mtmp", [128, 272], F32)
        mtmp2 = sb("mtmp2", [128, 272], F32)
        wid = sb("wid", [128, 16, 8], BF16)

        P = [ps("P%d" % i) for i in range(7)]
        PTR = ps("PTR", (128, 1024), BF16)

        def mm(out, lhsT, rhs, start, stop, reads, writes):
            return S.op("pe", lambda e: e.matmul(out, lhsT=lhsT, rhs=rhs, start=start, stop=stop),
                        reads=reads, writes=writes)

        def act(out, in_, func, reads, writes, scale=1.0, bias=0.0):
            return S.op("act", lambda e: e.activation(out=out, in_=in_, func=func, scale=scale, bias=bias),
                        reads=reads, writes=writes)

        def vcopy(eng, out, in_, reads, writes):
            return S.op(eng, lambda e: e.tensor_copy(out=out, in_=in_), reads=reads, writes=writes)

        def tt(out, in0, in1, op, reads, writes, eng="dve"):
            return S.op(eng, lambda e: e.tensor_tensor(out=out, in0=in0, in1=in1, op=op), reads=reads, writes=writes)

        def ts(out, in0, s1, op0, reads, writes, s2=None, op1=None, eng="dve", accum=None):
            def f(e):
                kw = {}
                if accum is not None:
                    kw["accum_out"] = accum
                if op1 is None:
                    return e.tensor_scalar(out=out, in0=in0, scalar1=s1, scalar2=None, op0=op0, **kw)
                return e.tensor_scalar(out=out, in0=in0, scalar1=s1, scalar2=s2, op0=op0, op1=op1, **kw)
            return S.op(eng, f, reads=reads, writes=writes)

        def stt(out, in0, scalar, in1, op0, op1, reads, writes):
            return S.op("dve", lambda e: e.scalar_tensor_tensor(out=out, in0=in0, scalar=scalar, in1=in1,
                                                                  op0=op0, op1=op1), reads=reads, writes=writes)

        wcount = [0]
        ucount = [0]
        wfirst = [True]
        wcache = {}

        def wload(src_ap, kc, mu):
            slot = wcount[0] % 3
            wcount[0] += 1
            uid = ucount[0]
            ucount[0] += 1
            n = kc * mu
            sname_ = "wbuf%d" % slot
            dst = wbuf[:, slot, 0:n].rearrange("p (k m) -> p k m", k=kc)
            if uid not in wcache:
                wcache[uid] = (nc.dram_tensor("wc%d" % uid, [128, n], BF16).ap(), n)
            cap, cn = wcache[uid]
            assert cn == n
            if wfirst[0]:
                S.dma("pool", dst, src_ap.rearrange("(k p) m -> p k m", p=128), writes=[sname_])
                S.dma("pool", cap[:, :], wbuf[:, slot, 0:n], reads=[sname_], writes=["wc%d" % uid], key="wcw%d" % slot)
            else:
                S.dma("pool", wbuf[:, slot, 0:n], cap[:, :], reads=["wc%d" % uid], writes=[sname_], key=sname_)
            return dst, sname_

        pcount = [0]

        def pbank():
            i = pcount[0] % 2
            pcount[0] += 1
            return P[5 + i], "P%d" % (5 + i)

        def ckpt(name, dump=None, dreads=()):
            if stop == name:
                if dump is not None and dbg_out is not None:
                    ap, shape = dump
                    S.dma("pool", dbg_out[0:shape[0], 0:shape[1]], ap, reads=list(dreads), writes=["dbg_d"])
                S.dead = True

        S.dma("sp", identf[:], ident_d[:, :], writes=["identf"])
        vcopy("dve", identb[:], identf[:], ["identf"], ["identb"])
        S.op("dve", lambda e: e.memset(onesf[:], 1.0), writes=["onesf"])
        S.op("dve", lambda e: e.memset(onesb[:], 1.0), writes=["onesb"])
        S.dma("sp", gall[:], g_all.rearrange("l p c -> p l c"), writes=["gall"])
        S.dma("sp", kvg[:], kvg_d.rearrange("l p c -> p l c"), writes=["kvg"])
        S.dma("sp", qg[:], qg_d.rearrange("l p c -> p l c"), writes=["qg"])
        S.dma("sp", kq[:], kq_d[:, :], writes=["kq"])
        S.dma("sp", cmask[:], cmask_d[:, :], writes=["cmask"])
        S.dma("sp", cmq[:], cmq_d[:, :], writes=["cmask"])
        S.dma("sp", bis[:], bis_d[:, :], writes=["bis"])
        S.dma("pool", jmb[:], jm_d[:, :], writes=["jmb"])
        for i in range(4):
            vcopy("dve", I4[:, i, :], identb[:], ["identb"], ["I4"])
        S.op("dve", lambda e: e.memset(swv[:], 1.0), writes=["swv0", "swv1"])
        for b_ in range(2):
            for g_ in range(2):
                S.op("dve", lambda e, b_=b_, g_=g_: e.memset(swv[0:32, b_, 0, g_ * 128:g_ * 128 + 64], 0.0),
                     writes=["swv%d" % b_])
        S.dma("sp", btab[0:32, :], btab_d[:, :], writes=["btab"])
        S.op("dve", lambda e: e.memset(btab[32:33, :], NEGB / 8.0), writes=["btab32"])
        S.dma("sp", isc[0:33, 0:NF], oh_d[:, :], writes=["isc"])
        for c0 in range(0, NF, 512):
            w_ = min(512, NF - c0)
            pb, pn = pbank()
            mm(pb[0:32, 0:w_], btab[0:33, :], isc[0:33, c0:c0 + w_], True, True, ["btab", "btab32", "isc"], [pn])
            ts(fsb[0:32, c0:c0 + w_], pb[0:32, 0:w_], 8.0, ALU.mult, [pn], ["fsb"])
        S.dma("sp", fscr.ap()[:, :], fsb[0:32, :], reads=["fsb"], writes=["fscr"])

        def toep(dst, nk, nq_, row0, base):
            src = bass.AP(tensor=fscr, offset=row0 * NF + base, ap=[[1, nk], [NF, 16], [1, nq_]])
            S.dma("sp", dst, src, reads=["fscr"], writes=["T"])

        for dl in range(2):
            toep(TA[:, dl, :, :], 128, 128, 0, FB_A0 + 255 * dl)
            toep(TB[:, dl, :, :], 128, 128, 16, FB_B0 + 255 * dl)
        S.op("dve", lambda e: e.memset(TmZ[0:32, :, :], 0.0), writes=["T"])
        S.op("dve", lambda e: e.memset(TmA0[0:32, :, :], 0.0), writes=["T"])
        S.op("dve", lambda e: e.memset(TqA[0:32, :, :], 0.0), writes=["T"])
        toep(TmA0[0:16, :, :], 16, 128, 0, FB_MA)
        toep(TmB0[0:16, :, :], 16, 128, 16, FB_MB)
        toep(TqA[0:16, :, :], 16, 16, 0, FB_QA)
        toep(TqB[0:16, :, :], 16, 16, 16, FB_QB)
        ckpt("consts")

        def layer_consts(l):
            S.dma("pool", wuk[:], w_uk_d[l].rearrange("(hp two) d r -> (two d) hp r", two=2), writes=["wuk"])
            for c_ in range(2):
                S.dma("pool", wuv[:, c_, :, :], w_uv_d[l][:, c_ * 128:(c_ + 1) * 128, :].rearrange("h p d -> p h d"),
                      writes=["wuv"])
            S.dma("sp", sk[0:1, :], sinks_d[l:l + 1, :], writes=["sk"])
            S.dma("sp", bsm[0:1, 0:16], btab_d[31:32, 0:16], writes=["bsm"])
            tt(sk[0:1, :], sk[0:1, :], bsm[0:1, 0:16], ALU.subtract, ["sk", "bsm"], ["sk"])
            ts(sk[0:1, :], sk[0:1, :], 8.0, ALU.mult, ["sk"], ["sk"])
            sk2 = PT[0:1, 0, 0:512]
            for hh in range(16):
                ts(PT[0:1, hh % 3, 0:128], onesf[0:1, :], sk[0:1, hh:hh + 1], ALU.mult, ["sk", "onesf"], ["PT%d" % (hh % 3)])
                S.dma("sp", sscr.ap()[0:1, hh, :], PT[0:1, hh % 3, 0:128], reads=["PT%d" % (hh % 3)], writes=["sscr"])
            S.dma("sp", TmZ[16:17, :, :], sscr.ap()[0:1, :, :], reads=["sscr"], writes=["T"])
            S.dma("sp", TmA0[16:17, :, :], sscr.ap()[0:1, :, :], reads=["sscr"], writes=["T"])
            S.dma("sp", TqA[16:17, :, :], sscr.ap()[0:1, :, 0:16], reads=["sscr"], writes=["T"])

        def rmsnorm_group(src_dram, c0, ncol, gvec, gname, sname):
            pb, pn = pbank()
            for c in range(16):
                slot = c % 3
                S.dma("sp", hrot[:, slot, 0:ncol], src_dram[c * 128:(c + 1) * 128, c0:c0 + ncol],
                      reads=[sname], writes=["hrot%d" % slot])
                act(sq[:, c % 2, 0:ncol], hrot[:, slot, 0:ncol], AF.Square, ["hrot%d" % slot], ["sq%d" % (c % 2)])
                mm(pb[:, 0:ncol], onesf[:], sq[:, c % 2, 0:ncol], c == 0, c == 15, ["onesf", "sq%d" % (c % 2)], [pn])
            act(rstd[:, 0:ncol], pb[:, 0:ncol], AF.Sqrt, [pn], ["rstd"], scale=1.0 / D, bias=1e-6)
            S.op("dve", lambda e: e.reciprocal(out=rstd[:, 0:ncol], in_=rstd[:, 0:ncol]), reads=["rstd"], writes=["rstd"])

        def subnorm(x32, nch, ncol, nfeat, gvec_fn, gname, out_fn, rname):
            pb, pn = pbank()
            for c in range(nch):
                act(sq[:, c % 2, 0:ncol], x32[:, c, 0:ncol], AF.Square, [rname], ["sq%d" % (c % 2)])
                mm(pb[:, 0:ncol], onesf[:], sq[:, c % 2, 0:ncol], c == 0, c == nch - 1, ["onesf", "sq%d" % (c % 2)], [pn])
            act(mtmp2[:, 0:ncol], pb[:, 0:ncol], AF.Sqrt, [pn], ["mtmp2"], scale=1.0 / nfeat, bias=1e-6)
            S.op("dve", lambda e: e.reciprocal(out=mtmp2[:, 0:ncol], in_=mtmp2[:, 0:ncol]), reads=["mtmp2"], writes=["mtmp2"])
            for c in range(nch):
                dst, dname = out_fn(c)
                stt(dst, x32[:, c, 0:ncol], gvec_fn(c), mtmp2[:, 0:ncol], ALU.mult, ALU.mult,
                    [rname, gname, "mtmp2"], [dname])

        def linear(w_ap, kc, m_total, rhs_fn, rhs_reads, ncol, evac, mu=256):
            mu = min(mu, m_total, 4096 // kc)
            for m0 in range(0, m_total, mu):
                mw = min(mu, m_total - m0)
                wt, wn = wload(w_ap[:, m0:m0 + mw], kc, mw)
                for j in range(0, mw, 128):
                    mwid = min(128, mw - j)
                    pb, pn = pbank()
                    for k in range(kc):
                        mm(pb[0:mwid, 0:ncol], wt[:, k, j:j + mwid], rhs_fn(k), k == 0, k == kc - 1, [wn] + rhs_reads, [pn])
                    evac((m0 + j) // 128, pb, pn, mwid)

        def kside_group(l, c0, ncol, slots):
            w_in = w_in_d[l]

            def ev_to(dst_fn):
                def ev(mc, pb, pn, mwid):
                    dst, dn = dst_fn(mc)
                    act(dst, pb[0:mwid, 0:ncol], AF.Copy, [pn], [dn])
                return ev
            ur = lambda k: u[:, k, 0:ncol]
            linear(w_in[:, O_KA:O_KA + 128], 16, 128, ur, ["u"], ncol, ev_to(lambda mc: (kso[:, 0:ncol], "kso")), mu=128)
            S.dma("sp", ksA[:, c0:c0 + ncol], kso[:, 0:ncol], reads=["kso"], writes=["ksA_d"])
            linear(w_in[:, O_VA:O_VA + 128], 16, 128, ur, ["u"], ncol, ev_to(lambda mc: (vfm[:, 0:ncol], "vfm")), mu=128)
            linear(w_in[:, O_CKV:O_CKV + 256], 16, 256, ur, ["u"], ncol,
                   ev_to(lambda mc: (ckv32[:, mc, 0:ncol], "ckv32")))
            for half in range(2):
                pass
            wt, wn = wload(w_in[:, O_KI:O_KI + 64], 16, 64)
            pb, pn = pbank()
            for half in range(2):
                for k in range(16):
                    mm(pb[half * 64:half * 64 + 64, 0:ncol], wt[:, k, :], u[:, k, 0:ncol], k == 0, k == 15, [wn, "u"], [pn])
            act(kfm[:, 2, c0:c0 + ncol], pb[:, 0:ncol], AF.Copy, [pn], ["kfm"])
            subnorm(ckv32, 2, ncol, 256, lambda c: kvg[:, l, c:c + 1], "kvg",
                    lambda c: (kfm[:, c, c0:c0 + ncol], "kfm"), "ckv32")
            for (s, loc, nq) in slots:
                tix = 0 if s == 32 else 1 + s
                srcs = [vfm[:, loc:loc + nq], kfm[:, 0, c0 + loc:c0 + loc + nq], kfm[:, 1, c0 + loc:c0 + loc + nq]]
                for i, src in enumerate(srcs):
                    S.op("pe", lambda e, src=src, i=i: e.transpose(PTR[0:nq, i * 128:(i + 1) * 128], src, identb[:]),
                         reads=["vfm", "kfm", "identb"], writes=["PTR"])
                vcopy("dve", kso_tm[0:nq, :], PTR[0:nq, 0:128], ["PTR"], ["kso_tm"])
                vcopy("dve", ktm[0:nq, tix, :], PTR[0:nq, 128:384], ["PTR"], ["ktm"])
                S.dma("sp", ksV[c0 + loc:c0 + loc + nq, :], kso_tm[0:nq, :], reads=["kso_tm"], writes=["ksV_d"])

        def swa_qblock(s, qc0, nq):
            buf = s % 2
            sk_, sv_ = "swk%d" % buf, "swv%d" % buf
            pos0 = 16 + 128 * s if s < 32 else 0
            pieces = [(0, 0, 16)]
            if s < 32:
                if s >= 1:
                    pieces.append((16, pos0 - 128, 256))
                else:
                    pieces.append((144, pos0, 128))
            for (dcol, scol, w_) in pieces:
                S.dma("sp", swk[:, buf, 0, dcol:dcol + w_], ksA[:, scol:scol + w_], reads=["ksA_d"], writes=[sk_])
                S.dma("sp", swk[0:64, buf, 1, dcol:dcol + w_], ksA[64:128, scol:scol + w_], reads=["ksA_d"], writes=[sk_])
                S.dma("sp", swk[64:128, buf, 1, dcol:dcol + w_], ksA[0:64, scol:scol + w_], reads=["ksA_d"], writes=[sk_])
            vt = [(0, 0, 16)]
            if s < 32:
                if s >= 1:
                    vt.append((1, pos0 - 128, 128))
                vt.append((2, pos0, 128))
            for (t_, row0, n_) in vt:
                S.dma("sp", swv[0:n_, buf, t_, :].rearrange("p (g c) -> p g c", g=2)[:, :, 0:64],
                      ksV[row0:row0 + n_, :].rearrange("p (g c) -> p g c", g=2), reads=["ksV_d"], writes=[sv_])
            if s == 32:
                tiles = [(0, 17, 16, TqA, None)]
            else:
                mt = TmA0 if s == 0 else TmZ
                tiles = [(0, 17, 16, mt, None)]
                if s >= 1:
                    tiles.append((1, 128, 128, TA, 1))
                tiles.append((2, 128, 128, TA, 0))
            for gi in range(4):
                g = gi // 2
                OA, DN = P[2], P[3]
                for ti, (tix, kb, kk, Tt, dl) in enumerate(tiles):
                    kcol = [0, 16, 144][tix]
                    pt = PT[:, ti % 3, :]
                    ptn = "PT%d" % (ti % 3)
                    jl = jmb[0:kb, 128:128 + kb] if kb == 17 else jmb[0:128, 0:128]
                    for par in range(2):
                        Sb = P[ti % 2] if par == 0 else P[5 + ti % 2]
                        sn = ("P%d" % (ti % 2)) if par == 0 else ("P%d" % (5 + ti % 2))
                        hsel = slice(4 * gi + par, 4 * gi + 4, 2)
                        rhs = Tt[0:kb, hsel, 0:nq] if dl is None else Tt[0:kb, dl, hsel, 0:nq]
                        mm(Sb[0:kb, 0:2 * nq].rearrange("p (a q) -> p a q", a=2), jl, rhs, True, False, ["jmb", "T"], [sn])
                        for a in range(2):
                            h = 4 * gi + 2 * a + par
                            pb_ = par * 64
                            v = 0 if (g == par) else 1
                            mm(Sb[0:kk, a * nq:(a + 1) * nq], swk[pb_:pb_ + 64, buf, v, kcol:kcol + kk],
                               pj[pb_:pb_ + 64, C_QA + h // 2, qc0:qc0 + nq], False, a == 1, [sk_, "pjqa"], [sn])
                        act(pt[0:kb, par * 2 * nq:(par + 1) * 2 * nq], Sb[0:kb, 0:2 * nq], AF.Exp, [sn], [ptn], scale=0.125)
                    first, last = ti == 0, ti == len(tiles) - 1
                    for par in range(2):
                        ptv = pt[0:kb, par * 2 * nq:(par + 1) * 2 * nq]
                        mm(OA[par * 64:par * 64 + 64, 0:2 * nq], swv[0:kb, buf, tix, g * 128:g * 128 + 64], ptv, first, last,
                           [sv_, ptn], ["P2"])
                        mm(DN[par * 64:par * 64 + 64, 0:2 * nq], swv[0:kb, buf, tix, g * 128 + 64:g * 128 + 128], ptv, first, last,
                           [sv_, ptn], ["P3"])
                S.op("dve", lambda e: e.reciprocal(out=rden[:, 0:2 * nq], in_=DN[:, 0:2 * nq]), reads=["P3"], writes=["rden"])
                tt(rden[:, 0:2 * nq], rden[:, 0:2 * nq], OA[:, 0:2 * nq], ALU.mult, ["rden", "P2"], ["rden"])
                for a in range(2):
                    dst = pj[:, C_ZA + 2 * gi + a, qc0:qc0 + nq]
                    tt(dst, dst, rden[:, a * nq:(a + 1) * nq], ALU.mult, ["pjza", "rden"], ["pjza"])

        def dsa_qblock(s, qc0, nq, wslot):
            ntr = 0 if s == 32 else s + 1
            K = 16 + 128 * ntr
            for k0 in range(0, K, 512):
                kw = min(512, K - k0)
                for hi in range(8):
                    pb_ = (hi % 2) * 64
                    pb, pn = pbank()
                    mm(pb[0:nq, 0:kw], pj[pb_:pb_ + 64, C_QI + hi // 2, qc0:qc0 + nq], kfm[pb_:pb_ + 64, 2, k0:k0 + kw],
                       True, True, ["pjqi", "kfm"], [pn])
                    rt = rtmp[:, hi % 2, :]
                    rn = "rtmp%d" % (hi % 2)
                    act(rt[0:nq, 0:kw], pb[0:nq, 0:kw], AF.Relu, [pn], [rn])
                    if hi == 0:
                        ts(isc[0:nq, k0:k0 + kw], rt[0:nq, 0:kw], wq[0:nq, wslot, 0:1], ALU.mult, [rn, "wq"], ["isc"])
                    else:
                        stt(isc[0:nq, k0:k0 + kw], rt[0:nq, 0:kw], wq[0:nq, wslot, hi:hi + 1], isc[0:nq, k0:k0 + kw],
                            ALU.mult, ALU.add, [rn, "wq", "isc"], ["isc"])
            if s == 32:
                n0, cm = 0, cmq[0:nq, 0:16]
            else:
                n0, cm = 16 + 128 * s, cmask[0:nq, 0:128]
            near = isc[0:nq, n0:K]
            nw = K - n0
            lo, hi_, mid, cnt, stp, w0, m1, m2 = [bsm[0:nq, i:i + 1] for i in range(8)]
            tt(rtmp[0:nq, 0, 0:nw], near, cm, ALU.subtract, ["isc", "cmask"], ["rtmp0"])
            S.op("dve", lambda e: e.tensor_reduce(out=m1, in_=rtmp[0:nq, 0, 0:nw], op=ALU.min, axis=AXX),
                 reads=["rtmp0"], writes=["bsm"])
            if n0 > 0:
                S.op("dve", lambda e: e.tensor_reduce(out=m2, in_=isc[0:nq, 0:n0], op=ALU.min, axis=AXX),
                     reads=["isc"], writes=["bsm"])
                tt(m1, m1, m2, ALU.min, ["bsm"], ["bsm"])
            tt(near, near, cm, ALU.add, ["isc", "cmask"], ["isc"])
            S.op("dve", lambda e: e.tensor_reduce(out=hi_, in_=isc[0:nq, 0:K], op=ALU.max, axis=AXX),
                 reads=["isc"], writes=["bsm"])
            vcopy("dve", lo, m1, ["bsm"], ["bsm"])
            stt(w0, hi_, 1e-3, lo, ALU.add, ALU.subtract, ["bsm"], ["bsm"])
            ts(wh[0:nq, :], bis[0:nq, :], w0, ALU.mult, ["bis", "bsm"], ["wh"])
            kqv = kq[0:nq, s:s + 1]
            for it in range(NBIS):
                tt(mid, lo, wh[0:nq, it:it + 1], ALU.add, ["bsm", "wh"], ["bsm"])
                ts(mneg[0:nq, 0:K], isc[0:nq, 0:K], mid, ALU.is_ge, ["isc", "bsm"], ["mneg", "bsm"],
                   s2=0.0, op1=ALU.add, accum=cnt)
                stt(stp, cnt, kqv, wh[0:nq, it:it + 1], ALU.is_ge, ALU.mult, ["bsm", "kq", "wh"], ["bsm"])
                tt(lo, lo, stp, ALU.add, ["bsm"], ["bsm"])
            ts(mneg[0:nq, 0:K], isc[0:nq, 0:K], lo, ALU.is_lt, ["isc", "bsm"], ["mneg"], s2=NEGB, op1=ALU.mult)
            ktiles = [(0, 0, 16)] + [(1 + t, 16 + 128 * t, 128) for t in range(ntr)]
            for gi in range(4):
                for c in range(2):
                    for par in range(2):
                        pb, pn = pbank()
                        pb_ = par * 64
                        for a in range(2):
                            h = 4 * gi + 2 * a + par
                            mm(pb[:, a * nq:(a + 1) * nq], wuk[pb_:pb_ + 64, h // 2, c * 128:(c + 1) * 128],
                               pj[pb_:pb_ + 64, C_QB + h // 2, qc0:qc0 + nq], True, True, ["wuk", "pjqb"], [pn])
                        for a in range(2):
                            vcopy("dve", qlat[:, c, 2 * a + par, 0:nq], pb[:, a * nq:(a + 1) * nq], [pn], ["qlat"])
                O0, O1, DN = P[2], P[3], P[4]
                for ti, (tix, kcol, kk) in enumerate(ktiles):
                    Sb = P[ti % 2]
                    sn = "P%d" % (ti % 2)
                    Sv = Sb[0:kk, 0:4 * nq].rearrange("p (h q) -> p h q", h=4)
                    extra = []
                    if s < 32 and tix >= 1:
                        dl = s - (tix - 1)
                        if dl in (0, 1):
                            extra.append((jmb[0:128, 0:128], TB[:, dl, 4 * gi:4 * gi + 4, 0:nq]))
                    if tix == 0 and s == 0:
                        extra.append((jmb[0:16, 128:144], TmB0[0:16, 4 * gi:4 * gi + 4, 0:nq]))
                    if s == 32:
                        extra.append((jmb[0:16, 128:144], TqB[0:16, 4 * gi:4 * gi + 4, 0:nq]))
                    for c in range(2):
                        mm(Sv, kfm[:, c, kcol:kcol + kk], qlat[:, c, :, 0:nq], c == 0, False, ["kfm", "qlat"], [sn])
                    mm(Sv, mneg[0:nq, kcol:kcol + kk], I4[0:nq, :, 0:nq], False, len(extra) == 0, ["mneg", "I4"], [sn])
                    for ei, (lt, rh) in enumerate(extra):
                        mm(Sv, lt, rh, False, ei == len(extra) - 1, ["jmb", "T"], [sn])
                    pt = PT[:, ti % 3, :]
                    ptn = "PT%d" % (ti % 3)
                    act(pt[0:kk, 0:4 * nq], Sb[0:kk, 0:4 * nq], AF.Exp, [sn], [ptn], scale=0.125)
                    first, last = ti == 0, ti == len(ktiles) - 1
                    mm(O0[:, 0:4 * nq], ktm[0:kk, tix, 0:128], pt[0:kk, 0:4 * nq], first, last, ["ktm", ptn], ["P2"])
                    mm(O1[:, 0:4 * nq], ktm[0:kk, tix, 128:256], pt[0:kk, 0:4 * nq], first, last, ["ktm", ptn], ["P3"])
                    mm(DN[:, 0:4 * nq], onesb[0:kk, :], pt[0:kk, 0:4 * nq], first, last, ["onesb", ptn], ["P4"])
                S.op("dve", lambda e: e.reciprocal(out=rden[:, 0:4 * nq], in_=DN[:, 0:4 * nq]), reads=["P4"], writes=["rden"])
                tt(olat[:, 0, :, 0:nq], O0[:, 0:4 * nq].rearrange("p (h q) -> p h q", h=4),
                   rden[:, 0:4 * nq].rearrange("p (h q) -> p h q", h=4), ALU.mult, ["P2", "rden"], ["olat"])
                tt(olat[:, 1, :, 0:nq], O1[:, 0:4 * nq].rearrange("p (h q) -> p h q", h=4),
                   rden[:, 0:4 * nq].rearrange("p (h q) -> p h q", h=4), ALU.mult, ["P3", "rden"], ["olat"])
                pb, pn = pbank()
                for hh in range(4):
                    h = 4 * gi + hh
                    a, par = hh // 2, hh % 2
                    for c in range(2):
                        mm(pb[par * 64:par * 64 + 64, a * nq:(a + 1) * nq], wuv[:, c, h, :], olat[:, c, hh, 0:nq],
                           c == 0, c == 1, ["wuv", "olat"], [pn])
                for a in range(2):
                    dst = pj[:, C_ZB + 2 * gi + a, qc0:qc0 + nq]
                    tt(dst, dst, pb[:, a * nq:(a + 1) * nq], ALU.mult, ["pjzb", pn], ["pjzb"])

        def layer_group(l, hsrc, sname, hdst, dname, c0, ncol, slots):
            w_in = w_in_d[l]
            rmsnorm_group(hsrc, c0, ncol, None, None, sname)
            for c in range(16):
                slot = c % 3
                S.dma("sp", hrot[:, slot, 0:ncol], hsrc[c * 128:(c + 1) * 128, c0:c0 + ncol],
                      reads=[sname], writes=["hrot%d" % slot])
                stt(u[:, c, 0:ncol], hrot[:, slot, 0:ncol], gall[:, l, c:c + 1], rstd[:, 0:ncol], ALU.mult, ALU.mult,
                    ["hrot%d" % slot, "gall", "rstd"], ["u"])
            kside_group(l, c0, ncol, slots)

            def ev_copy(chunk0, name):
                def ev(mc, pb, pn, mwid):
                    act(pj[0:mwid, chunk0 + mc, 0:ncol], pb[0:mwid, 0:ncol], AF.Copy, [pn], [name])
                return ev

            def ev_silu(chunk0, name):
                def ev(mc, pb, pn, mwid):
                    act(pj[0:mwid, chunk0 + mc, 0:ncol], pb[0:mwid, 0:ncol], AF.Silu, [pn], [name])
                return ev

            def ev_cq(mc, pb, pn, mwid):
                vcopy("dve", cq32[:, mc, 0:ncol], pb[:, 0:ncol], [pn], ["cq32"])

            ur = lambda k: u[:, k, 0:ncol]
            linear(w_in[:, O_QA:O_QA + 1024], 16, 1024, ur, ["u"], ncol, ev_copy(C_QA, "pjqa"))
            linear(w_in[:, O_ZA:O_ZA + 1024], 16, 1024, ur, ["u"], ncol, ev_silu(C_ZA, "pjza"))
            linear(w_in[:, O_ZB:O_ZB + 1024], 16, 1024, ur, ["u"], ncol, ev_silu(C_ZB, "pjzb"))
            linear(w_in[:, O_CQ:O_CQ + 512], 16, 512, ur, ["u"], ncol, ev_cq)
            S.dma("pool", wid[:], w_in[:, O_WI:O_WI + 8].rearrange("(k p) m -> p k m", p=128), writes=["wid"])
            for ti, (s, loc, nq) in enumerate(slots):
                pb, pn = pbank()
                for k in range(16):
                    mm(pb[0:nq, 0:8], u[:, k, loc:loc + nq], wid[:, k, :], k == 0, k == 15, ["u", "wid"], [pn])
                ts(wq[0:nq, ti, :], pb[0:nq, 0:8], (8.0 ** -0.5) * (64.0 ** -0.5), ALU.mult, [pn], ["wq"])
            subnorm(cq32, 4, ncol, 512, lambda c: qg[:, l, c:c + 1], "qg", lambda c: (pj[:, C_CQN + c, 0:ncol], "pjcqn"), "cq32")
            cr = lambda k: pj[:, C_CQN + k, 0:ncol]
            linear(w_qb_d[l], 4, 1024, cr, ["pjcqn"], ncol, ev_copy(C_QB, "pjqb"), mu=1024)
            linear(w_iq_d[l], 4, 512, cr, ["pjcqn"], ncol, ev_copy(C_QI, "pjqi"), mu=512)
            ckpt("B%d" % l)
            for ti, (s, loc, nq) in enumerate(slots):
                swa_qblock(s, loc, nq)
                dsa_qblock(s, loc, nq, ti)
            ckpt("C%d" % l)

            def mgc(m):
                return C_QA + m if m < 8 else C_QB + (m - 8)

            for m in range(16):
                pa, pb2 = P[5], P[6]
                wa, wan = wload(w_pa_d[l][:, m * 128:(m + 1) * 128], 8, 128)
                for k in range(8):
                    mm(pa[:, 0:ncol], wa[:, k, :], pj[:, C_ZA + k, 0:ncol], k == 0, k == 7, [wan, "pjza"], ["P5"])
                vcopy("dve", gsig[:, 0, 0:ncol], pa[:, 0:ncol], ["P5"], ["gsig0"])
                wb_, wbn = wload(w_pb_d[l][:, m * 128:(m + 1) * 128], 8, 128)
                for k in range(8):
                    mm(pb2[:, 0:ncol], wb_[:, k, :], pj[:, C_ZB + k, 0:ncol], k == 0, k == 7, [wbn, "pjzb"], ["P6"])
                vcopy("dve", gsig[:, 1, 0:ncol], pb2[:, 0:ncol], ["P6"], ["gsig1"])
                wg, wgn = wload(w_in[:, O_GA + m * 128:O_GA + (m + 1) * 128], 16, 128)
                for k in range(16):
                    mm(pa[:, 0:ncol], wg[:, k, :], u[:, k, 0:ncol], k == 0, k == 15, [wgn, "u"], ["P5"])
                act(mtmp[:, 0:ncol], pa[:, 0:ncol], AF.Sigmoid, ["P5"], ["mtmp"])
                tt(gsig[:, 0, 0:ncol], gsig[:, 0, 0:ncol], mtmp[:, 0:ncol], ALU.mult, ["gsig0", "mtmp"], ["gsig0"])
                wg, wgn = wload(w_in[:, O_GB + m * 128:O_GB + (m + 1) * 128], 16, 128)
                for k in range(16):
                    mm(pb2[:, 0:ncol], wg[:, k, :], u[:, k, 0:ncol], k == 0, k == 15, [wgn, "u"], ["P6"])
                act(mtmp[:, 0:ncol], pb2[:, 0:ncol], AF.Sigmoid, ["P6"], ["mtmp"])
                tt(gsig[:, 1, 0:ncol], gsig[:, 1, 0:ncol], mtmp[:, 0:ncol], ALU.mult, ["gsig1", "mtmp"], ["gsig1"])
                tt(pj[:, mgc(m), 0:ncol], gsig[:, 0, 0:ncol], gsig[:, 1, 0:ncol], ALU.add, ["gsig0", "gsig1"], ["pjqa", "pjqb"])

            def ev_res(mc, pb, pn, mwid):
                slot = mc % 3
                S.dma("sp", hrot[:, slot, 0:ncol], hsrc[mc * 128:(mc + 1) * 128, c0:c0 + ncol],
                      reads=[sname], writes=["hrot%d" % slot])
                tt(hrot[:, slot, 0:ncol], hrot[:, slot, 0:ncol], pb[:, 0:ncol], ALU.add, ["hrot%d" % slot, pn], ["hrot%d" % slot])
                S.dma("sp", hdst[mc * 128:(mc + 1) * 128, c0:c0 + ncol], hrot[:, slot, 0:ncol],
                      reads=["hrot%d" % slot], writes=[dname])
            linear(w_out_d[l], 16, D, lambda k: pj[:, mgc(k), 0:ncol], ["pjqa", "pjqb"], ncol, ev_res)

        def final_group(hsrc, sname, c0, ncol):
            lo_ = max(c0, 16)
            n_ = c0 + ncol - lo_
            rmsnorm_group(hsrc, lo_, n_, None, None, sname)
            for c in range(16):
                slot = c % 3
                S.dma("sp", hrot[:, slot, 0:n_], hsrc[c * 128:(c + 1) * 128, lo_:lo_ + n_], reads=[sname], writes=["hrot%d" % slot])
                stt(hrot[:, slot, 0:n_], hrot[:, slot, 0:n_], gall[:, DEPTH, c:c + 1], rstd[:, 0:n_], ALU.mult, ALU.mult,
                    ["hrot%d" % slot, "gall", "rstd"], ["hrot%d" % slot])
                S.dma("sp", final_out[c * 128:(c + 1) * 128, lo_ - 16:lo_ - 16 + n_], hrot[:, slot, 0:n_],
                      reads=["hrot%d" % slot], writes=["final"])

        for l in range(nlayer):
            layer_consts(l)
            hsrc, sname = (hT, "hT_d") if l == 0 else (hbuf[l % 2], "hbuf%d" % (l % 2))
            hdst, dname = hbuf[(l + 1) % 2], "hbuf%d" % ((l + 1) % 2)
            for gi_, (c0, ncol, slots) in enumerate(groups):
                ucount[0] = 0
                wfirst[0] = (gi_ == 0)
                layer_group(l, hsrc, sname, hdst, dname, c0, ncol, slots)
        hsrc, sname = hbuf[nlayer % 2], "hbuf%d" % (nlayer % 2)
        for (c0, ncol, slots) in groups:
            final_group(hsrc, sname, c0, ncol)
        S.finish("sp")
        build.n_ops = {k: v.count for k, v in S.engs.items()}
    return nc


_NC = {}


def kernel(x, meta_tokens, bias_table, norm_g, w_in, q_norm_g, kv_norm_g, w_qb, w_iq,
           w_uk, w_uv, sinks, w_proj_a, w_proj_b, w_out, final_g):
    f = lambda a: np.ascontiguousarray(np.asarray(a, np.float32))
    x, meta_tokens, bias_table, norm_g, w_in = f(x), f(meta_tokens), f(bias_table), f(norm_g), f(w_in)
    q_norm_g, kv_norm_g, w_qb, w_iq, w_uk, w_uv = f(q_norm_g), f(kv_norm_g), f(w_qb), f(w_iq), f(w_uk), f(w_uv)
    sinks, w_proj_a, w_proj_b, w_out, final_g = f(sinks), f(w_proj_a), f(w_proj_b), f(w_out), f(final_g)
    cores = list(range(8))
    t = _tables()
    hTs = [np.ascontiguousarray(np.concatenate([meta_tokens, x[b]], 0).T) for b in range(2)]
    g_all = np.stack([_pc(norm_g[l], 16) for l in range(DEPTH)] + [_pc(final_g, 16)], 0)
    shared = dict(ident=t["ident"], g_all=g_all, w_in=w_in,
                  qg=np.stack([_pc(q_norm_g[l], 4) for l in range(DEPTH)], 0),
                  kvg=np.stack([_pc(kv_norm_g[l], 2) for l in range(DEPTH)], 0),
                  w_qb=w_qb, w_iq=w_iq, w_uk=w_uk, w_uv=w_uv, sinks=sinks, w_pa=w_proj_a, w_pb=w_proj_b,
                  w_out=w_out, btab=bias_table, cmask=t["cmask"], kq=t["kq"], cmq=t["cmq"], oh=t["oh"],
                  bis=t["bis"], jm=t["jm"])
    if "nc" not in _NC:
        _NC["nc"] = build()
    maps = []
    for c in cores:
        m = dict(shared)
        m["hT"] = hTs[c // 4]
        maps.append(m)
    res = run_bass_kernel_spmd(_NC["nc"], maps, core_ids=cores).results
    out = np.stack([np.ascontiguousarray(np.asarray(res[4 * b]["out"]).T) for b in range(2)], 0)
    return out.astype(np.float32)
```

```python
import math
from contextlib import ExitStack
import numpy as np
import ml_dtypes
import concourse.bass as bass
import concourse.mybir as mybir
from concourse.bass_utils import run_bass_kernel_spmd

F32, BF16 = mybir.dt.float32, mybir.dt.bfloat16
ALU = mybir.AluOpType
AF = mybir.ActivationFunctionType
AXX = mybir.AxisListType.X
NPBF = ml_dtypes.bfloat16

D = 2048
SEQ = 4096
NM = 16
L = SEQ + NM
NT = 4112
DEPTH = 4
NB = 32
TOPK = 256
NEGB = -30000.0
O_QA, O_KA, O_VA, O_ZA, O_CQ, O_CKV, O_ZB, O_KI, O_WI, O_GA, O_GB = (
    0, 1024, 1152, 1280, 2304, 2816, 3072, 4096, 4160, 4168, 6216)
GROUPS = [(0, 272, [(32, 0, 16), (0, 16, 128), (1, 144, 128)])] + [(16 + 256 * g, 256, [(2 * g, 0, 128), (2 * g + 1, 128, 128)]) for g in range(1, 16)]
NLAYER_BUILD = DEPTH
C_QA, C_ZA, C_ZB, C_QB, C_QI, C_CQN = 0, 8, 16, 24, 32, 36
NPJ = 40
FB_A0, FB_A1, FB_B0, FB_B1, FB_MA, FB_MB, FB_QA, FB_QB = 0, 255, 510, 765, 1020, 1163, 1306, 1337
NF = 1368
NBIS = 16


class _Eng:
    def __init__(self, name, h, sem):
        self.name, self.h, self.sem, self.count, self.waited = name, h, sem, 0, {}


class Sync:
    def __init__(self, nc, st):
        self.nc, self.st = nc, st
        self.engs = {}
        for name, h in (("pe", nc.tensor), ("act", nc.scalar), ("dve", nc.vector),
                        ("pool", nc.gpsimd), ("sp", nc.sync)):
            self.engs[name] = _Eng(name, h, st.enter_context(nc.semaphore("sem_" + name)))
        self.last_w, self.readers, self.dsems = {}, {}, {}
        self.dead = False
        self.nrot = 0

    def _dsem(self, key):
        if key not in self.dsems:
            self.dsems[key] = [self.st.enter_context(self.nc.semaphore("ds%d" % len(self.dsems))), 0]
        return self.dsems[key]

    def _deps(self, reads, writes):
        deps = []
        for r in reads:
            t = self.last_w.get(r)
            if t is not None:
                deps.append(t)
        for w in writes:
            t = self.last_w.get(w)
            if t is not None:
                deps.append(t)
            deps.extend(self.readers.get(w, ()))
        return deps

    def _wait(self, e, deps, is_dma=False):
        best = {}
        for (sem, val, src) in deps:
            if (not is_dma) and e.name == "pe" and src == "pe":
                continue
            k = id(sem)
            if k not in best or best[k][1] < val:
                best[k] = (sem, val)
        for k, (sem, val) in best.items():
            if e.waited.get(k, 0) < val:
                e.h.wait_ge(sem, val)
                e.waited[k] = val

    def _commit(self, tok, reads, writes):
        for r in reads:
            self.readers.setdefault(r, []).append(tok)
        for w in writes:
            self.last_w[w] = tok
            self.readers[w] = []

    def op(self, eng, fn, reads=(), writes=()):
        if self.dead:
            return None
        e = self.engs[eng]
        self._wait(e, self._deps(reads, writes))
        if e.count >= 30000:
            e.sem = self.st.enter_context(self.nc.semaphore("sem_%s_%d" % (e.name, len(self.dsems) + self.nrot)))
            self.nrot += 1
            e.count = 0
        ins = fn(e.h)
        e.count += 1
        ins.then_inc(e.sem, 1)
        self._commit((e.sem, e.count, eng), reads, writes)
        return ins

    def dma(self, eng, out, in_, reads=(), writes=(), key=None, **kw):
        if self.dead:
            return None
        e = self.engs[eng]
        self._wait(e, self._deps(reads, writes), is_dma=True)
        ds = self._dsem(key if key is not None else writes[0])
        ins = e.h.dma_start(out=out, in_=in_, **kw)
        ds[1] += 16
        ins.then_inc(ds[0], 16)
        self._commit((ds[0], ds[1], "dma"), reads, writes)
        return ins

    def finish(self, eng="sp"):
        deps = list(self.last_w.values())
        for v in self.readers.values():
            deps.extend(v)
        self._wait(self.engs[eng], deps, is_dma=True)


def _bucket(d):
    d = np.maximum(d, 0)
    nf = np.maximum(d, 1).astype(np.float32)
    large = 16 + (np.log(nf / np.float32(16)) / np.float32(math.log(128 / 16)) * np.float32(16)).astype(np.int32)
    large = np.minimum(large, 31)
    return np.where(d < 16, d, large)


def _oh_cols(dvals, allowed):
    n = len(dvals)
    oh = np.zeros((33, n), np.float32)
    b = _bucket(np.asarray(dvals))
    for i in range(n):
        if allowed[i]:
            oh[b[i], i] += 1.0
            oh[31, i] -= 1.0
        else:
            oh[32, i] = 1.0
    return oh


def _tables():
    j = np.arange(-127, 128)
    cols = []
    cols.append(_oh_cols(j, (j >= 0) & (j < 128)))
    cols.append(_oh_cols(128 + j, (128 + j >= 0) & (128 + j < 128)))
    cols.append(_oh_cols(j, j >= 0))
    cols.append(_oh_cols(128 + j, 128 + j >= 0))
    jm_ = np.arange(-15, 128)
    dm = 16 + jm_
    cols.append(_oh_cols(dm, np.ones_like(dm, bool)))
    cols.append(_oh_cols(dm, np.ones_like(dm, bool)))
    jq = np.arange(-15, 16)
    cols.append(_oh_cols(jq, jq >= 0))
    cols.append(_oh_cols(jq, jq >= 0))
    oh = np.concatenate(cols, 1)
    assert oh.shape[1] == NF
    qq = np.arange(128)[:, None]
    kk = np.arange(128)[None, :]
    cm = np.where(qq - kk >= 0, 0.0, -1e30).astype(np.float32)
    kq = np.zeros((128, 33), np.float32)
    for s in range(32):
        pos = 16 + 128 * s + np.arange(128)
        kq[:, s] = np.minimum(TOPK, pos + 1)
    kq[:, 32] = 1.0
    kq[:16, 32] = np.arange(16) + 1
    cmq = np.zeros((128, 16), np.float32)
    cmq[:16] = np.where(np.arange(16)[:, None] >= np.arange(16)[None, :], 0.0, -1e30)
    eye = np.eye(128, dtype=np.float32)
    anti = np.ascontiguousarray(eye[::-1])
    jm = np.zeros((128, 256), np.float32)
    jm[:, 0:128] = anti
    for i in range(16):
        jm[i, 128 + 15 - i] = 1.0
    jm[16, 128 + 16] = 1.0
    bis = np.tile((0.5 ** (np.arange(NBIS) + 1)).astype(np.float32)[None, :], (128, 1))
    return dict(oh=oh, cmask=cm, kq=kq, cmq=cmq, ident=eye, bis=bis, jm=jm)


def _pc(v, n):
    return np.ascontiguousarray(np.asarray(v, np.float32).reshape(n, 128).T)


def build(nlayer=DEPTH, dbg=False, stop=None, groups=None):
    nc = bass.Bass("TRN2", target_bir_lowering=False)
    groups = GROUPS if groups is None else groups

    def din(name, shape, dt=F32):
        return nc.dram_tensor(name, list(shape), dt, kind="ExternalInput").ap()

    def dout(name, shape, dt=F32):
        return nc.dram_tensor(name, list(shape), dt, kind="ExternalOutput").ap()

    hT = din("hT", [D, NT])
    ident_d = din("ident", [128, 128])
    g_all = din("g_all", [DEPTH + 1, 128, 16])
    w_in_d = din("w_in", [DEPTH, D, 8264])
    qg_d = din("qg", [DEPTH, 128, 4])
    kvg_d = din("kvg", [DEPTH, 128, 2])
    w_qb_d = din("w_qb", [DEPTH, 512, 1024])
    w_iq_d = din("w_iq", [DEPTH, 512, 512])
    w_uk_d = din("w_uk", [DEPTH, 16, 64, 256])
    w_uv_d = din("w_uv", [DEPTH, 16, 256, 64])
    sinks_d = din("sinks", [DEPTH, 16])
    w_pa_d = din("w_pa", [DEPTH, 1024, D])
    w_pb_d = din("w_pb", [DEPTH, 1024, D])
    w_out_d = din("w_out", [DEPTH, D, D])
    btab_d = din("btab", [32, 32])
    cmask_d = din("cmask", [128, 128])
    kq_d = din("kq", [128, 33])
    cmq_d = din("cmq", [128, 16])
    oh_d = din("oh", [33, NF])
    bis_d = din("bis", [128, NBIS])
    jm_d = din("jm", [128, 256])
    final_out = dout("out", [D, SEQ])
    fscr = nc.dram_tensor("fscr", [32, NF], BF16)
    sscr = nc.dram_tensor("sscr", [16, 16, 128], BF16)
    hbuf = [nc.dram_tensor("hbuf%d" % i, [D, NT], F32).ap() for i in range(2)]
    ksA = nc.dram_tensor("ksA", [128, NT], BF16).ap()
    ksV = nc.dram_tensor("ksV", [NT, 128], BF16).ap()
    dbg_out = dout("dbg", [128, 4112]) if dbg else None

    st = ExitStack()
    with st:
        S = Sync(nc, st)

        def sb(name, shape, dt):
            return st.enter_context(nc.sbuf_tensor("s_" + name, list(shape), dt))

        def ps(name, shape=(128, 512), dt=F32):
            return st.enter_context(nc.psum_tensor(name, list(shape), dt))

        hrot = sb("hrot", [128, 3, 272], F32)
        u = sb("u", [128, 16, 272], BF16)
        rstd = sb("rstd", [128, 272], F32)
        sq = sb("sq", [128, 2, 272], F32)
        wbuf = sb("wbuf", [128, 4, 4096], BF16)
        identf = sb("identf", [128, 128], F32)
        identb = sb("identb", [128, 128], BF16)
        onesf = sb("onesf", [128, 128], F32)
        onesb = sb("onesb", [128, 128], BF16)
        gall = sb("gall", [128, DEPTH + 1, 16], F32)
        kvg = sb("kvg", [128, DEPTH, 2], F32)
        qg = sb("qg", [128, DEPTH, 4], F32)
        kso = sb("kso", [128, 272], BF16)
        kso_tm = sb("kso_tm", [128, 128], BF16)
        ckv32 = sb("ckv32", [128, 2, 272], F32)
        vfm = sb("vfm", [128, 272], BF16)
        pj = sb("pj", [128, NPJ, 272], BF16)
        cq32 = sb("cq32", [128, 4, 272], F32)
        kfm = sb("kfm", [128, 3, L], BF16)
        ktm = sb("ktm", [128, 33, 256], BF16)
        swk = sb("swk", [128, 2, 2, 272], BF16)
        swv = sb("swv", [128, 2, 3, 256], BF16)
        isc = sb("isc", [128, L], F32)
        mneg = sb("mneg", [128, L], BF16)
        TA = sb("TA", [128, 2, 16, 128], BF16)
        TB = sb("TB", [128, 2, 16, 128], BF16)
        TmA0 = sb("TmA0", [128, 16, 128], BF16)
        TmB0 = sb("TmB0", [128, 16, 128], BF16)
        TmZ = sb("TmZ", [128, 16, 128], BF16)
        TqA = sb("TqA", [128, 16, 16], BF16)
        TqB = sb("TqB", [128, 16, 16], BF16)
        I4 = sb("I4", [128, 4, 128], BF16)
        jmb = sb("jmb", [128, 256], BF16)
        qlat = sb("qlat", [128, 2, 4, 128], BF16)
        olat = sb("olat", [128, 2, 4, 128], BF16)
        PT = sb("PT", [128, 3, 512], BF16)
        rtmp = sb("rtmp", [128, 2, 512], F32)
        rden = sb("rden", [128, 512], F32)
        wuk = sb("wuk", [128, 8, 256], BF16)
        wuv = sb("wuv", [128, 2, 16, 64], BF16)
        wq = sb("wq", [128, 3, 8], F32)
        kq = sb("kq", [128, 33], F32)
        cmask = sb("cmask", [128, 128], F32)
        cmq = sb("cmq", [128, 16], F32)
        bis = sb("bis", [128, NBIS], F32)
        btab = sb("btab", [128, 32], F32)
        fsb = sb("fsb", [128, NF], BF16)
        sk = sb("sk", [128, 16], F32)
        bsm = sb("bsm", [128, 16], F32)
        wh = sb("wh", [128, NBIS], F32)
        gsig = sb("gsig", [128, 2, 272], F32)
        mtmp = sb("mtmp", [128, 272], F32)
        mtmp2 = sb("mtmp2", [128, 272], F32)
        wid = sb("wid", [128, 16, 8], BF16)

        P = [ps("P%d" % i) for i in range(7)]
        PTR = ps("PTR", (128, 1024), BF16)

        def mm(out, lhsT, rhs, start, stop, reads, writes):
            return S.op("pe", lambda e: e.matmul(out, lhsT=lhsT, rhs=rhs, start=start, stop=stop),
                        reads=reads, writes=writes)

        def act(out, in_, func, reads, writes, scale=1.0, bias=0.0):
            return S.op("act", lambda e: e.activation(out=out, in_=in_, func=func, scale=scale, bias=bias),
                        reads=reads, writes=writes)

        def vcopy(eng, out, in_, reads, writes):
            return S.op(eng, lambda e: e.tensor_copy(out=out, in_=in_), reads=reads, writes=writes)

        def tt(out, in0, in1, op, reads, writes, eng="dve"):
            return S.op(eng, lambda e: e.tensor_tensor(out=out, in0=in0, in1=in1, op=op), reads=reads, writes=writes)

        def ts(out, in0, s1, op0, reads, writes, s2=None, op1=None, eng="dve", accum=None):
            def f(e):
                kw = {}
                if accum is not None:
                    kw["accum_out"] = accum
                if op1 is None:
                    return e.tensor_scalar(out=out, in0=in0, scalar1=s1, scalar2=None, op0=op0, **kw)
                return e.tensor_scalar(out=out, in0=in0, scalar1=s1, scalar2=s2, op0=op0, op1=op1, **kw)
            return S.op(eng, f, reads=reads, writes=writes)

        def stt(out, in0, scalar, in1, op0, op1, reads, writes):
            return S.op("dve", lambda e: e.scalar_tensor_tensor(out=out, in0=in0, scalar=scalar, in1=in1,
                                                                  op0=op0, op1=op1), reads=reads, writes=writes)

        wcount = [0]
        ucount = [0]
        wfirst = [True]
        wcache = {}

        def wload(src_ap, kc, mu):
            slot = wcount[0] % 4
            wcount[0] += 1
            uid = ucount[0]
            ucount[0] += 1
            n = kc * mu
            sname_ = "wbuf%d" % slot
            dst = wbuf[:, slot, 0:n].rearrange("p (k m) -> p k m", k=kc)
            if uid not in wcache:
                wcache[uid] = (nc.dram_tensor("wc%d" % uid, [128, n], BF16).ap(), n)
            cap, cn = wcache[uid]
            assert cn == n
            if wfirst[0]:
                S.dma("pool", dst, src_ap.rearrange("(k p) m -> p k m", p=128), writes=[sname_])
                S.dma("pool", cap[:, :], wbuf[:, slot, 0:n], reads=[sname_], writes=["wc%d" % uid], key="wcw%d" % slot)
            else:
                S.dma("pool", wbuf[:, slot, 0:n], cap[:, :], reads=["wc%d" % uid], writes=[sname_], key=sname_)
            return dst, sname_

        pcount = [0]

        def pbank():
            i = pcount[0] % 2
            pcount[0] += 1
            return P[5 + i], "P%d" % (5 + i)

        def ckpt(name, dump=None, dreads=()):
            if stop == name:
                if dump is not None and dbg_out is not None:
                    ap, shape = dump
                    S.dma("pool", dbg_out[0:shape[0], 0:shape[1]], ap, reads=list(dreads), writes=["dbg_d"])
                S.dead = True

        S.dma("sp", identf[:], ident_d[:, :], writes=["identf"])
        vcopy("dve", identb[:], identf[:], ["identf"], ["identb"])
        S.op("dve", lambda e: e.memset(onesf[:], 1.0), writes=["onesf"])
        S.op("dve", lambda e: e.memset(onesb[:], 1.0), writes=["onesb"])
        S.dma("sp", gall[:], g_all.rearrange("l p c -> p l c"), writes=["gall"])
        S.dma("sp", kvg[:], kvg_d.rearrange("l p c -> p l c"), writes=["kvg"])
        S.dma("sp", qg[:], qg_d.rearrange("l p c -> p l c"), writes=["qg"])
        S.dma("sp", kq[:], kq_d[:, :], writes=["kq"])
        S.dma("sp", cmask[:], cmask_d[:, :], writes=["cmask"])
        S.dma("sp", cmq[:], cmq_d[:, :], writes=["cmask"])
        S.dma("sp", bis[:], bis_d[:, :], writes=["bis"])
        S.dma("pool", jmb[:], jm_d[:, :], writes=["jmb"])
        for i in range(4):
            vcopy("dve", I4[:, i, :], identb[:], ["identb"], ["I4"])
        S.op("dve", lambda e: e.memset(swv[:], 1.0), writes=["swv0", "swv1"])
        for b_ in range(2):
            for g_ in range(2):
                S.op("dve", lambda e, b_=b_, g_=g_: e.memset(swv[0:32, b_, 0, g_ * 128:g_ * 128 + 64], 0.0),
                     writes=["swv%d" % b_])
        S.dma("sp", btab[0:32, :], btab_d[:, :], writes=["btab"])
        S.op("dve", lambda e: e.memset(btab[32:33, :], NEGB / 8.0), writes=["btab32"])
        S.dma("sp", isc[0:33, 0:NF], oh_d[:, :], writes=["isc"])
        for c0 in range(0, NF, 512):
            w_ = min(512, NF - c0)
            pb, pn = pbank()
            mm(pb[0:32, 0:w_], btab[0:33, :], isc[0:33, c0:c0 + w_], True, True, ["btab", "btab32", "isc"], [pn])
            ts(fsb[0:32, c0:c0 + w_], pb[0:32, 0:w_], 8.0, ALU.mult, [pn], ["fsb"])
        S.dma("sp", fscr.ap()[:, :], fsb[0:32, :], reads=["fsb"], writes=["fscr"])

        def toep(dst, nk, nq_, row0, base):
            src = bass.AP(tensor=fscr, offset=row0 * NF + base, ap=[[1, nk], [NF, 16], [1, nq_]])
            S.dma("sp", dst, src, reads=["fscr"], writes=["T"])

        for dl in range(2):
            toep(TA[:, dl, :, :], 128, 128, 0, FB_A0 + 255 * dl)
            toep(TB[:, dl, :, :], 128, 128, 16, FB_B0 + 255 * dl)
        S.op("dve", lambda e: e.memset(TmZ[0:32, :, :], 0.0), writes=["T"])
        S.op("dve", lambda e: e.memset(TmA0[0:32, :, :], 0.0), writes=["T"])
        S.op("dve", lambda e: e.memset(TqA[0:32, :, :], 0.0), writes=["T"])
        toep(TmA0[0:16, :, :], 16, 128, 0, FB_MA)
        toep(TmB0[0:16, :, :], 16, 128, 16, FB_MB)
        toep(TqA[0:16, :, :], 16, 16, 0, FB_QA)
        toep(TqB[0:16, :, :], 16, 16, 16, FB_QB)
        ckpt("consts")

        def layer_consts(l):
            S.dma("pool", wuk[:], w_uk_d[l].rearrange("(hp two) d r -> (two d) hp r", two=2), writes=["wuk"])
            for c_ in range(2):
                S.dma("pool", wuv[:, c_, :, :], w_uv_d[l][:, c_ * 128:(c_ + 1) * 128, :].rearrange("h p d -> p h d"),
                      writes=["wuv"])
            S.dma("sp", sk[0:1, :], sinks_d[l:l + 1, :], writes=["sk"])
            S.dma("sp", bsm[0:1, 0:16], btab_d[31:32, 0:16], writes=["bsm"])
            tt(sk[0:1, :], sk[0:1, :], bsm[0:1, 0:16], ALU.subtract, ["sk", "bsm"], ["sk"])
            ts(sk[0:1, :], sk[0:1, :], 8.0, ALU.mult, ["sk"], ["sk"])
            sk2 = PT[0:1, 0, 0:512]
            for hh in range(16):
                ts(PT[0:1, hh % 3, 0:128], onesf[0:1, :], sk[0:1, hh:hh + 1], ALU.mult, ["sk", "onesf"], ["PT%d" % (hh % 3)])
                S.dma("sp", sscr.ap()[0:1, hh, :], PT[0:1, hh % 3, 0:128], reads=["PT%d" % (hh % 3)], writes=["sscr"])
            S.dma("sp", TmZ[16:17, :, :], sscr.ap()[0:1, :, :], reads=["sscr"], writes=["T"])
            S.dma("sp", TmA0[16:17, :, :], sscr.ap()[0:1, :, :], reads=["sscr"], writes=["T"])
            S.dma("sp", TqA[16:17, :, :], sscr.ap()[0:1, :, 0:16], reads=["sscr"], writes=["T"])

        def rmsnorm_group(src_dram, c0, ncol, gvec, gname, sname):
            pb, pn = pbank()
            for c in range(16):
                slot = c % 3
                S.dma("sp", hrot[:, slot, 0:ncol], src_dram[c * 128:(c + 1) * 128, c0:c0 + ncol],
                      reads=[sname], writes=["hrot%d" % slot])
                act(sq[:, c % 2, 0:ncol], hrot[:, slot, 0:ncol], AF.Square, ["hrot%d" % slot], ["sq%d" % (c % 2)])
                mm(pb[:, 0:ncol], onesf[:], sq[:, c % 2, 0:ncol], c == 0, c == 15, ["onesf", "sq%d" % (c % 2)], [pn])
            act(rstd[:, 0:ncol], pb[:, 0:ncol], AF.Sqrt, [pn], ["rstd"], scale=1.0 / D, bias=1e-6)
            S.op("dve", lambda e: e.reciprocal(out=rstd[:, 0:ncol], in_=rstd[:, 0:ncol]), reads=["rstd"], writes=["rstd"])

        def subnorm(x32, nch, ncol, nfeat, gvec_fn, gname, out_fn, rname):
            pb, pn = pbank()
            for c in range(nch):
                act(sq[:, c % 2, 0:ncol], x32[:, c, 0:ncol], AF.Square, [rname], ["sq%d" % (c % 2)])
                mm(pb[:, 0:ncol], onesf[:], sq[:, c % 2, 0:ncol], c == 0, c == nch - 1, ["onesf", "sq%d" % (c % 2)], [pn])
            act(mtmp2[:, 0:ncol], pb[:, 0:ncol], AF.Sqrt, [pn], ["mtmp2"], scale=1.0 / nfeat, bias=1e-6)
            S.op("dve", lambda e: e.reciprocal(out=mtmp2[:, 0:ncol], in_=mtmp2[:, 0:ncol]), reads=["mtmp2"], writes=["mtmp2"])
            for c in range(nch):
                dst, dname = out_fn(c)
                stt(dst, x32[:, c, 0:ncol], gvec_fn(c), mtmp2[:, 0:ncol], ALU.mult, ALU.mult,
                    [rname, gname, "mtmp2"], [dname])

        def linear(w_ap, kc, m_total, rhs_fn, rhs_reads, ncol, evac, mu=256):
            mu = min(mu, m_total, 4096 // kc)
            for m0 in range(0, m_total, mu):
                mw = min(mu, m_total - m0)
                wt, wn = wload(w_ap[:, m0:m0 + mw], kc, mw)
                for j in range(0, mw, 128):
                    mwid = min(128, mw - j)
                    pb, pn = pbank()
                    for k in range(kc):
                        mm(pb[0:mwid, 0:ncol], wt[:, k, j:j + mwid], rhs_fn(k), k == 0, k == kc - 1, [wn] + rhs_reads, [pn])
                    evac((m0 + j) // 128, pb, pn, mwid)

        def kside_group(l, c0, ncol, slots):
            w_in = w_in_d[l]

            def ev_to(dst_fn):
                def ev(mc, pb, pn, mwid):
                    dst, dn = dst_fn(mc)
                    act(dst, pb[0:mwid, 0:ncol], AF.Copy, [pn], [dn])
                return ev
            ur = lambda k: u[:, k, 0:ncol]
            linear(w_in[:, O_KA:O_KA + 128], 16, 128, ur, ["u"], ncol, ev_to(lambda mc: (kso[:, 0:ncol], "kso")), mu=128)
            S.dma("sp", ksA[:, c0:c0 + ncol], kso[:, 0:ncol], reads=["kso"], writes=["ksA_d"])
            linear(w_in[:, O_VA:O_VA + 128], 16, 128, ur, ["u"], ncol, ev_to(lambda mc: (vfm[:, 0:ncol], "vfm")), mu=128)
            linear(w_in[:, O_CKV:O_CKV + 256], 16, 256, ur, ["u"], ncol,
                   ev_to(lambda mc: (ckv32[:, mc, 0:ncol], "ckv32")))
            for half in range(2):
                pass
            wt, wn = wload(w_in[:, O_KI:O_KI + 64], 16, 64)
            pb, pn = pbank()
            for half in range(2):
                for k in range(16):
                    mm(pb[half * 64:half * 64 + 64, 0:ncol], wt[:, k, :], u[:, k, 0:ncol], k == 0, k == 15, [wn, "u"], [pn])
            act(kfm[:, 2, c0:c0 + ncol], pb[:, 0:ncol], AF.Copy, [pn], ["kfm"])
            subnorm(ckv32, 2, ncol, 256, lambda c: kvg[:, l, c:c + 1], "kvg",
                    lambda c: (kfm[:, c, c0:c0 + ncol], "kfm"), "ckv32")
            for (s, loc, nq) in slots:
                tix = 0 if s == 32 else 1 + s
                srcs = [vfm[:, loc:loc + nq], kfm[:, 0, c0 + loc:c0 + loc + nq], kfm[:, 1, c0 + loc:c0 + loc + nq]]
                for i, src in enumerate(srcs):
                    S.op("pe", lambda e, src=src, i=i: e.transpose(PTR[0:nq, i * 128:(i + 1) * 128], src, identb[:]),
                         reads=["vfm", "kfm", "identb"], writes=["PTR"])
                vcopy("dve", kso_tm[0:nq, :], PTR[0:nq, 0:128], ["PTR"], ["kso_tm"])
                vcopy("dve", ktm[0:nq, tix, :], PTR[0:nq, 128:384], ["PTR"], ["ktm"])
                S.dma("sp", ksV[c0 + loc:c0 + loc + nq, :], kso_tm[0:nq, :], reads=["kso_tm"], writes=["ksV_d"])

        def swa_qblock(s, qc0, nq):
            buf = s % 2
            sk_, sv_ = "swk%d" % buf, "swv%d" % buf
            pos0 = 16 + 128 * s if s < 32 else 0
            pieces = [(0, 0, 16)]
            if s < 32:
                if s >= 1:
                    pieces.append((16, pos0 - 128, 256))
                else:
                    pieces.append((144, pos0, 128))
            for (dcol, scol, w_) in pieces:
                S.dma("sp", swk[:, buf, 0, dcol:dcol + w_], ksA[:, scol:scol + w_], reads=["ksA_d"], writes=[sk_])
                S.dma("sp", swk[0:64, buf, 1, dcol:dcol + w_], ksA[64:128, scol:scol + w_], reads=["ksA_d"], writes=[sk_])
                S.dma("sp", swk[64:128, buf, 1, dcol:dcol + w_], ksA[0:64, scol:scol + w_], reads=["ksA_d"], writes=[sk_])
            vt = [(0, 0, 16)]
            if s < 32:
                if s >= 1:
                    vt.append((1, pos0 - 128, 128))
                vt.append((2, pos0, 128))
            for (t_, row0, n_) in vt:
                S.dma("sp", swv[0:n_, buf, t_, :].rearrange("p (g c) -> p g c", g=2)[:, :, 0:64],
                      ksV[row0:row0 + n_, :].rearrange("p (g c) -> p g c", g=2), reads=["ksV_d"], writes=[sv_])
            if s == 32:
                tiles = [(0, 17, 16, TqA, None)]
            else:
                mt = TmA0 if s == 0 else TmZ
                tiles = [(0, 17, 16, mt, None)]
                if s >= 1:
                    tiles.append((1, 128, 128, TA, 1))
                tiles.append((2, 128, 128, TA, 0))
            for gi in range(4):
                g = gi // 2
                OA, DN = P[2], P[3]
                def emit_S(ti):
                    tix, kb, kk, Tt, dl = tiles[ti]
                    kcol = [0, 16, 144][tix]
                    pt = PT[:, ti % 3, :]
                    ptn = "PT%d" % (ti % 3)
                    jl = jmb[0:kb, 128:128 + kb] if kb == 17 else jmb[0:128, 0:128]
                    for par in range(2):
                        Sb = P[ti % 2] if par == 0 else P[5 + ti % 2]
                        sn = ("P%d" % (ti % 2)) if par == 0 else ("P%d" % (5 + ti % 2))
                        hsel = slice(4 * gi + par, 4 * gi + 4, 2)
                        rhs = Tt[0:kb, hsel, 0:nq] if dl is None else Tt[0:kb, dl, hsel, 0:nq]
                        mm(Sb[0:kb, 0:2 * nq].rearrange("p (a q) -> p a q", a=2), jl, rhs, True, False, ["jmb", "T"], [sn])
                        for a in range(2):
                            h = 4 * gi + 2 * a + par
                            pb_ = par * 64
                            v = 0 if (g == par) else 1
                            mm(Sb[0:kk, a * nq:(a + 1) * nq], swk[pb_:pb_ + 64, buf, v, kcol:kcol + kk],
                               pj[pb_:pb_ + 64, C_QA + h // 2, qc0:qc0 + nq], False, a == 1, [sk_, "pjqa"], [sn])
                        act(pt[0:kb, par * 2 * nq:(par + 1) * 2 * nq], Sb[0:kb, 0:2 * nq], AF.Exp, [sn], [ptn], scale=0.125)

                def emit_PV(ti):
                    tix, kb, kk, Tt, dl = tiles[ti]
                    pt = PT[:, ti % 3, :]
                    ptn = "PT%d" % (ti % 3)
                    first, last = ti == 0, ti == len(tiles) - 1
                    for par in range(2):
                        ptv = pt[0:kb, par * 2 * nq:(par + 1) * 2 * nq]
                        mm(OA[par * 64:par * 64 + 64, 0:2 * nq], swv[0:kb, buf, tix, g * 128:g * 128 + 64], ptv, first, last,
                           [sv_, ptn], ["P2"])
                        mm(DN[par * 64:par * 64 + 64, 0:2 * nq], swv[0:kb, buf, tix, g * 128 + 64:g * 128 + 128], ptv, first, last,
                           [sv_, ptn], ["P3"])

                emit_S(0)
                for ti in range(len(tiles)):
                    if ti + 1 < len(tiles):
                        emit_S(ti + 1)
                    emit_PV(ti)
                S.op("dve", lambda e: e.reciprocal(out=rden[:, 0:2 * nq], in_=DN[:, 0:2 * nq]), reads=["P3"], writes=["rden"])
                tt(rden[:, 0:2 * nq], rden[:, 0:2 * nq], OA[:, 0:2 * nq], ALU.mult, ["rden", "P2"], ["rden"])
                for a in range(2):
                    dst = pj[:, C_ZA + 2 * gi + a, qc0:qc0 + nq]
                    tt(dst, dst, rden[:, a * nq:(a + 1) * nq], ALU.mult, ["pjza", "rden"], ["pjza"])

        def dsa_qblock(s, qc0, nq, wslot):
            ntr = 0 if s == 32 else s + 1
            K = 16 + 128 * ntr
            for k0 in range(0, K, 512):
                kw = min(512, K - k0)
                for hi in range(8):
                    pb_ = (hi % 2) * 64
                    pb, pn = pbank()
                    mm(pb[0:nq, 0:kw], pj[pb_:pb_ + 64, C_QI + hi // 2, qc0:qc0 + nq], kfm[pb_:pb_ + 64, 2, k0:k0 + kw],
                       True, True, ["pjqi", "kfm"], [pn])
                    rt = rtmp[:, hi % 2, :]
                    rn = "rtmp%d" % (hi % 2)
                    act(rt[0:nq, 0:kw], pb[0:nq, 0:kw], AF.Relu, [pn], [rn])
                    if hi == 0:
                        ts(isc[0:nq, k0:k0 + kw], rt[0:nq, 0:kw], wq[0:nq, wslot, 0:1], ALU.mult, [rn, "wq"], ["isc"])
                    else:
                        stt(isc[0:nq, k0:k0 + kw], rt[0:nq, 0:kw], wq[0:nq, wslot, hi:hi + 1], isc[0:nq, k0:k0 + kw],
                            ALU.mult, ALU.add, [rn, "wq", "isc"], ["isc"])
            if s == 32:
                n0, cm = 0, cmq[0:nq, 0:16]
            else:
                n0, cm = 16 + 128 * s, cmask[0:nq, 0:128]
            near = isc[0:nq, n0:K]
            nw = K - n0
            lo, hi_, mid, cnt, stp, w0, m1, m2 = [bsm[0:nq, i:i + 1] for i in range(8)]
            tt(rtmp[0:nq, 0, 0:nw], near, cm, ALU.subtract, ["isc", "cmask"], ["rtmp0"])
            S.op("dve", lambda e: e.tensor_reduce(out=m1, in_=rtmp[0:nq, 0, 0:nw], op=ALU.min, axis=AXX),
                 reads=["rtmp0"], writes=["bsm"])
            if n0 > 0:
                S.op("dve", lambda e: e.tensor_reduce(out=m2, in_=isc[0:nq, 0:n0], op=ALU.min, axis=AXX),
                     reads=["isc"], writes=["bsm"])
                tt(m1, m1, m2, ALU.min, ["bsm"], ["bsm"])
            tt(near, near, cm, ALU.add, ["isc", "cmask"], ["isc"])
            S.op("dve", lambda e: e.tensor_reduce(out=hi_, in_=isc[0:nq, 0:K], op=ALU.max, axis=AXX),
                 reads=["isc"], writes=["bsm"])
            vcopy("dve", lo, m1, ["bsm"], ["bsm"])
            stt(w0, hi_, 1e-3, lo, ALU.add, ALU.subtract, ["bsm"], ["bsm"])
            ts(wh[0:nq, :], bis[0:nq, :], w0, ALU.mult, ["bis", "bsm"], ["wh"])
            kqv = kq[0:nq, s:s + 1]
            for it in range(NBIS):
                tt(mid, lo, wh[0:nq, it:it + 1], ALU.add, ["bsm", "wh"], ["bsm"])
                ts(mneg[0:nq, 0:K], isc[0:nq, 0:K], mid, ALU.is_ge, ["isc", "bsm"], ["mneg", "bsm"],
                   s2=0.0, op1=ALU.add, accum=cnt)
                stt(stp, cnt, kqv, wh[0:nq, it:it + 1], ALU.is_ge, ALU.mult, ["bsm", "kq", "wh"], ["bsm"])
                tt(lo, lo, stp, ALU.add, ["bsm"], ["bsm"])
            ts(mneg[0:nq, 0:K], isc[0:nq, 0:K], lo, ALU.is_lt, ["isc", "bsm"], ["mneg"], s2=NEGB, op1=ALU.mult)
            ktiles = [(0, 0, 16)] + [(1 + t, 16 + 128 * t, 128) for t in range(ntr)]
            for gi in range(4):
                for c in range(2):
                    for par in range(2):
                        pb, pn = pbank()
                        pb_ = par * 64
                        for a in range(2):
                            h = 4 * gi + 2 * a + par
                            mm(pb[:, a * nq:(a + 1) * nq], wuk[pb_:pb_ + 64, h // 2, c * 128:(c + 1) * 128],
                               pj[pb_:pb_ + 64, C_QB + h // 2, qc0:qc0 + nq], True, True, ["wuk", "pjqb"], [pn])
                        for a in range(2):
                            act(qlat[:, c, 2 * a + par, 0:nq], pb[:, a * nq:(a + 1) * nq], AF.Copy, [pn], ["qlat"])
                O0, O1, DN = P[2], P[3], P[4]

                def emit_S(ti):
                    tix, kcol, kk = ktiles[ti]
                    Sb = P[ti % 2]
                    sn = "P%d" % (ti % 2)
                    Sv = Sb[0:kk, 0:4 * nq].rearrange("p (h q) -> p h q", h=4)
                    extra = []
                    if s < 32 and tix >= 1:
                        dl = s - (tix - 1)
                        if dl in (0, 1):
                            extra.append((jmb[0:128, 0:128], TB[:, dl, 4 * gi:4 * gi + 4, 0:nq]))
                    if tix == 0 and s == 0:
                        extra.append((jmb[0:16, 128:144], TmB0[0:16, 4 * gi:4 * gi + 4, 0:nq]))
                    if s == 32:
                        extra.append((jmb[0:16, 128:144], TqB[0:16, 4 * gi:4 * gi + 4, 0:nq]))
                    for c in range(2):
                        mm(Sv, kfm[:, c, kcol:kcol + kk], qlat[:, c, :, 0:nq], c == 0, False, ["kfm", "qlat"], [sn])
                    mm(Sv, mneg[0:nq, kcol:kcol + kk], I4[0:nq, :, 0:nq], False, len(extra) == 0, ["mneg", "I4"], [sn])
                    for ei, (lt, rh) in enumerate(extra):
                        mm(Sv, lt, rh, False, ei == len(extra) - 1, ["jmb", "T"], [sn])
                    pt = PT[:, ti % 3, :]
                    act(pt[0:kk, 0:4 * nq], Sb[0:kk, 0:4 * nq], AF.Exp, [sn], ["PT%d" % (ti % 3)], scale=0.125)

                def emit_PV(ti):
                    tix, kcol, kk = ktiles[ti]
                    pt = PT[:, ti % 3, :]
                    ptn = "PT%d" % (ti % 3)
                    first, last = ti == 0, ti == len(ktiles) - 1
                    mm(O0[:, 0:4 * nq], ktm[0:kk, tix, 0:128], pt[0:kk, 0:4 * nq], first, last, ["ktm", ptn], ["P2"])
                    mm(O1[:, 0:4 * nq], ktm[0:kk, tix, 128:256], pt[0:kk, 0:4 * nq], first, last, ["ktm", ptn], ["P3"])
                    mm(DN[:, 0:4 * nq], onesb[0:kk, :], pt[0:kk, 0:4 * nq], first, last, ["onesb", ptn], ["P4"])

                emit_S(0)
                for ti in range(len(ktiles)):
                    if ti + 1 < len(ktiles):
                        emit_S(ti + 1)
                    emit_PV(ti)
                S.op("dve", lambda e: e.reciprocal(out=rden[:, 0:4 * nq], in_=DN[:, 0:4 * nq]), reads=["P4"], writes=["rden"])
                tt(olat[:, 0, :, 0:nq], O0[:, 0:4 * nq].rearrange("p (h q) -> p h q", h=4),
                   rden[:, 0:4 * nq].rearrange("p (h q) -> p h q", h=4), ALU.mult, ["P2", "rden"], ["olat"])
                tt(olat[:, 1, :, 0:nq], O1[:, 0:4 * nq].rearrange("p (h q) -> p h q", h=4),
                   rden[:, 0:4 * nq].rearrange("p (h q) -> p h q", h=4), ALU.mult, ["P3", "rden"], ["olat"])
                pb, pn = pbank()
                for hh in range(4):
                    h = 4 * gi + hh
                    a, par = hh // 2, hh % 2
                    for c in range(2):
                        mm(pb[par * 64:par * 64 + 64, a * nq:(a + 1) * nq], wuv[:, c, h, :], olat[:, c, hh, 0:nq],
                           c == 0, c == 1, ["wuv", "olat"], [pn])
                for a in range(2):
                    dst = pj[:, C_ZB + 2 * gi + a, qc0:qc0 + nq]
                    tt(dst, dst, pb[:, a * nq:(a + 1) * nq], ALU.mult, ["pjzb", pn], ["pjzb"])

        def layer_group(l, hsrc, sname, hdst, dname, c0, ncol, slots):
            w_in = w_in_d[l]
            rmsnorm_group(hsrc, c0, ncol, None, None, sname)
            for c in range(16):
                slot = c % 3
                S.dma("sp", hrot[:, slot, 0:ncol], hsrc[c * 128:(c + 1) * 128, c0:c0 + ncol],
                      reads=[sname], writes=["hrot%d" % slot])
                stt(u[:, c, 0:ncol], hrot[:, slot, 0:ncol], gall[:, l, c:c + 1], rstd[:, 0:ncol], ALU.mult, ALU.mult,
                    ["hrot%d" % slot, "gall", "rstd"], ["u"])
            kside_group(l, c0, ncol, slots)

            def ev_copy(chunk0, name):
                def ev(mc, pb, pn, mwid):
                    act(pj[0:mwid, chunk0 + mc, 0:ncol], pb[0:mwid, 0:ncol], AF.Copy, [pn], [name])
                return ev

            def ev_silu(chunk0, name):
                def ev(mc, pb, pn, mwid):
                    act(pj[0:mwid, chunk0 + mc, 0:ncol], pb[0:mwid, 0:ncol], AF.Silu, [pn], [name])
                return ev

            def ev_cq(mc, pb, pn, mwid):
                vcopy("dve", cq32[:, mc, 0:ncol], pb[:, 0:ncol], [pn], ["cq32"])

            ur = lambda k: u[:, k, 0:ncol]
            linear(w_in[:, O_QA:O_QA + 1024], 16, 1024, ur, ["u"], ncol, ev_copy(C_QA, "pjqa"))
            linear(w_in[:, O_ZA:O_ZA + 1024], 16, 1024, ur, ["u"], ncol, ev_silu(C_ZA, "pjza"))
            linear(w_in[:, O_ZB:O_ZB + 1024], 16, 1024, ur, ["u"], ncol, ev_silu(C_ZB, "pjzb"))
            linear(w_in[:, O_CQ:O_CQ + 512], 16, 512, ur, ["u"], ncol, ev_cq)
            S.dma("pool", wid[:], w_in[:, O_WI:O_WI + 8].rearrange("(k p) m -> p k m", p=128), writes=["wid"])
            for ti, (s, loc, nq) in enumerate(slots):
                pb, pn = pbank()
                for k in range(16):
                    mm(pb[0:nq, 0:8], u[:, k, loc:loc + nq], wid[:, k, :], k == 0, k == 15, ["u", "wid"], [pn])
                ts(wq[0:nq, ti, :], pb[0:nq, 0:8], (8.0 ** -0.5) * (64.0 ** -0.5), ALU.mult, [pn], ["wq"])
            subnorm(cq32, 4, ncol, 512, lambda c: qg[:, l, c:c + 1], "qg", lambda c: (pj[:, C_CQN + c, 0:ncol], "pjcqn"), "cq32")
            cr = lambda k: pj[:, C_CQN + k, 0:ncol]
            linear(w_qb_d[l], 4, 1024, cr, ["pjcqn"], ncol, ev_copy(C_QB, "pjqb"), mu=1024)
            linear(w_iq_d[l], 4, 512, cr, ["pjcqn"], ncol, ev_copy(C_QI, "pjqi"), mu=512)
            ckpt("B%d" % l)
            for ti, (s, loc, nq) in enumerate(slots):
                swa_qblock(s, loc, nq)
                dsa_qblock(s, loc, nq, ti)
            ckpt("C%d" % l)

            def mgc(m):
                return C_QA + m if m < 8 else C_QB + (m - 8)

            for m in range(16):
                pa, pb2 = P[5], P[6]
                wa, wan = wload(w_pa_d[l][:, m * 128:(m + 1) * 128], 8, 128)
                for k in range(8):
                    mm(pa[:, 0:ncol], wa[:, k, :], pj[:, C_ZA + k, 0:ncol], k == 0, k == 7, [wan, "pjza"], ["P5"])
                vcopy("dve", gsig[:, 0, 0:ncol], pa[:, 0:ncol], ["P5"], ["gsig0"])
                wb_, wbn = wload(w_pb_d[l][:, m * 128:(m + 1) * 128], 8, 128)
                for k in range(8):
                    mm(pb2[:, 0:ncol], wb_[:, k, :], pj[:, C_ZB + k, 0:ncol], k == 0, k == 7, [wbn, "pjzb"], ["P6"])
                vcopy("dve", gsig[:, 1, 0:ncol], pb2[:, 0:ncol], ["P6"], ["gsig1"])
                wg, wgn = wload(w_in[:, O_GA + m * 128:O_GA + (m + 1) * 128], 16, 128)
                for k in range(16):
                    mm(pa[:, 0:ncol], wg[:, k, :], u[:, k, 0:ncol], k == 0, k == 15, [wgn, "u"], ["P5"])
                act(mtmp[:, 0:ncol], pa[:, 0:ncol], AF.Sigmoid, ["P5"], ["mtmp"])
                tt(gsig[:, 0, 0:ncol], gsig[:, 0, 0:ncol], mtmp[:, 0:ncol], ALU.mult, ["gsig0", "mtmp"], ["gsig0"])
                wg, wgn = wload(w_in[:, O_GB + m * 128:O_GB + (m + 1) * 128], 16, 128)
                for k in range(16):
                    mm(pb2[:, 0:ncol], wg[:, k, :], u[:, k, 0:ncol], k == 0, k == 15, [wgn, "u"], ["P6"])
                act(mtmp[:, 0:ncol], pb2[:, 0:ncol], AF.Sigmoid, ["P6"], ["mtmp"])
                tt(gsig[:, 1, 0:ncol], gsig[:, 1, 0:ncol], mtmp[:, 0:ncol], ALU.mult, ["gsig1", "mtmp"], ["gsig1"])
                tt(pj[:, mgc(m), 0:ncol], gsig[:, 0, 0:ncol], gsig[:, 1, 0:ncol], ALU.add, ["gsig0", "gsig1"], ["pjqa", "pjqb"])

            def ev_res(mc, pb, pn, mwid):
                slot = mc % 3
                S.dma("sp", hrot[:, slot, 0:ncol], hsrc[mc * 128:(mc + 1) * 128, c0:c0 + ncol],
                      reads=[sname], writes=["hrot%d" % slot])
                tt(hrot[:, slot, 0:ncol], hrot[:, slot, 0:ncol], pb[:, 0:ncol], ALU.add, ["hrot%d" % slot, pn], ["hrot%d" % slot])
                S.dma("sp", hdst[mc * 128:(mc + 1) * 128, c0:c0 + ncol], hrot[:, slot, 0:ncol],
                      reads=["hrot%d" % slot], writes=[dname])
            linear(w_out_d[l], 16, D, lambda k: pj[:, mgc(k), 0:ncol], ["pjqa", "pjqb"], ncol, ev_res)

        def final_group(hsrc, sname, c0, ncol):
            lo_ = max(c0, 16)
            n_ = c0 + ncol - lo_
            rmsnorm_group(hsrc, lo_, n_, None, None, sname)
            for c in range(16):
                slot = c % 3
                S.dma("sp", hrot[:, slot, 0:n_], hsrc[c * 128:(c + 1) * 128, lo_:lo_ + n_], reads=[sname], writes=["hrot%d" % slot])
                stt(hrot[:, slot, 0:n_], hrot[:, slot, 0:n_], gall[:, DEPTH, c:c + 1], rstd[:, 0:n_], ALU.mult, ALU.mult,
                    ["hrot%d" % slot, "gall", "rstd"], ["hrot%d" % slot])
                S.dma("sp", final_out[c * 128:(c + 1) * 128, lo_ - 16:lo_ - 16 + n_], hrot[:, slot, 0:n_],
                      reads=["hrot%d" % slot], writes=["final"])

        for l in range(nlayer):
            layer_consts(l)
            hsrc, sname = (hT, "hT_d") if l == 0 else (hbuf[l % 2], "hbuf%d" % (l % 2))
            hdst, dname = hbuf[(l + 1) % 2], "hbuf%d" % ((l + 1) % 2)
            for gi_, (c0, ncol, slots) in enumerate(groups):
                ucount[0] = 0
                wfirst[0] = (gi_ == 0)
                layer_group(l, hsrc, sname, hdst, dname, c0, ncol, slots)
        hsrc, sname = hbuf[nlayer % 2], "hbuf%d" % (nlayer % 2)
        for (c0, ncol, slots) in groups:
            final_group(hsrc, sname, c0, ncol)
        S.finish("sp")
        build.n_ops = {k: v.count for k, v in S.engs.items()}
    return nc


_NC = {}


def kernel(x, meta_tokens, bias_table, norm_g, w_in, q_norm_g, kv_norm_g, w_qb, w_iq,
           w_uk, w_uv, sinks, w_proj_a, w_proj_b, w_out, final_g):
    f = lambda a: np.ascontiguousarray(np.asarray(a, np.float32))
    x, meta_tokens, bias_table, norm_g, w_in = f(x), f(meta_tokens), f(bias_table), f(norm_g), f(w_in)
    q_norm_g, kv_norm_g, w_qb, w_iq, w_uk, w_uv = f(q_norm_g), f(kv_norm_g), f(w_qb), f(w_iq), f(w_uk), f(w_uv)
    sinks, w_proj_a, w_proj_b, w_out, final_g = f(sinks), f(w_proj_a), f(w_proj_b), f(w_out), f(final_g)
    cores = list(range(8))
    t = _tables()
    hTs = [np.ascontiguousarray(np.concatenate([meta_tokens, x[b]], 0).T) for b in range(2)]
    g_all = np.stack([_pc(norm_g[l], 16) for l in range(DEPTH)] + [_pc(final_g, 16)], 0)
    shared = dict(ident=t["ident"], g_all=g_all, w_in=w_in,
                  qg=np.stack([_pc(q_norm_g[l], 4) for l in range(DEPTH)], 0),
                  kvg=np.stack([_pc(kv_norm_g[l], 2) for l in range(DEPTH)], 0),
                  w_qb=w_qb, w_iq=w_iq, w_uk=w_uk, w_uv=w_uv, sinks=sinks, w_pa=w_proj_a, w_pb=w_proj_b,
                  w_out=w_out, btab=bias_table, cmask=t["cmask"], kq=t["kq"], cmq=t["cmq"], oh=t["oh"],
                  bis=t["bis"], jm=t["jm"])
    if "nc" not in _NC:
        _NC["nc"] = build()
    maps = []
    for c in cores:
        m = dict(shared)
        m["hT"] = hTs[c // 4]
        maps.append(m)
    res = run_bass_kernel_spmd(_NC["nc"], maps, core_ids=cores).results
    out = np.stack([np.ascontiguousarray(np.asarray(res[4 * b]["out"]).T) for b in range(2)], 0)
    return out.astype(np.float32)
```

```python
import math
from contextlib import ExitStack
import numpy as np
import ml_dtypes
import concourse.bass as bass
import concourse.mybir as mybir
from concourse.bass_utils import run_bass_kernel_spmd

F32, BF16 = mybir.dt.float32, mybir.dt.bfloat16
ALU = mybir.AluOpType
AF = mybir.ActivationFunctionType
AXX = mybir.AxisListType.X
NPBF = ml_dtypes.bfloat16

D = 2048
SEQ = 4096
NM = 16
L = SEQ + NM
NT = 4112
DEPTH = 4
NB = 32
TOPK = 256
NEGB = -30000.0
O_QA, O_KA, O_VA, O_ZA, O_CQ, O_CKV, O_ZB, O_KI, O_WI, O_GA, O_GB = (
    0, 1024, 1152, 1280, 2304, 2816, 3072, 4096, 4160, 4168, 6216)
GROUPS = [(0, 272, [(32, 0, 16), (0, 16, 128), (1, 144, 128)])] + [(16 + 256 * g, 256, [(2 * g, 0, 128), (2 * g + 1, 128, 128)]) for g in range(1, 16)]
NLAYER_BUILD = DEPTH
C_QA, C_ZA, C_ZB, C_QB, C_QI, C_CQN = 0, 8, 16, 24, 32, 36
NPJ = 40
FB_A0, FB_A1, FB_B0, FB_B1, FB_MA, FB_MB, FB_QA, FB_QB = 0, 255, 510, 765, 1020, 1163, 1306, 1337
NF = 1368
NBIS = 16


class _Eng:
    def __init__(self, name, h, sem):
        self.name, self.h, self.sem, self.count, self.waited = name, h, sem, 0, {}


class Sync:
    def __init__(self, nc, st):
        self.nc, self.st = nc, st
        self.engs = {}
        for name, h in (("pe", nc.tensor), ("act", nc.scalar), ("dve", nc.vector),
                        ("pool", nc.gpsimd), ("sp", nc.sync)):
            self.engs[name] = _Eng(name, h, st.enter_context(nc.semaphore("sem_" + name)))
        self.last_w, self.readers, self.dsems = {}, {}, {}
        self.dead = False
        self.nrot = 0

    def _dsem(self, key):
        if key not in self.dsems:
            self.dsems[key] = [self.st.enter_context(self.nc.semaphore("ds%d" % len(self.dsems))), 0]
        return self.dsems[key]

    def _deps(self, reads, writes):
        deps = []
        for r in reads:
            t = self.last_w.get(r)
            if t is not None:
                deps.append(t)
        for w in writes:
            t = self.last_w.get(w)
            if t is not None:
                deps.append(t)
            deps.extend(self.readers.get(w, ()))
        return deps

    def _wait(self, e, deps, is_dma=False):
        best = {}
        for (sem, val, src) in deps:
            if (not is_dma) and e.name == "pe" and src == "pe":
                continue
            k = id(sem)
            if k not in best or best[k][1] < val:
                best[k] = (sem, val)
        for k, (sem, val) in best.items():
            if e.waited.get(k, 0) < val:
                e.h.wait_ge(sem, val)
                e.waited[k] = val

    def _commit(self, tok, reads, writes):
        for r in reads:
            self.readers.setdefault(r, []).append(tok)
        for w in writes:
            self.last_w[w] = tok
            self.readers[w] = []

    def op(self, eng, fn, reads=(), writes=()):
        if self.dead:
            return None
        e = self.engs[eng]
        self._wait(e, self._deps(reads, writes))
        if e.count >= 30000:
            e.sem = self.st.enter_context(self.nc.semaphore("sem_%s_%d" % (e.name, len(self.dsems) + self.nrot)))
            self.nrot += 1
            e.count = 0
        ins = fn(e.h)
        e.count += 1
        ins.then_inc(e.sem, 1)
        self._commit((e.sem, e.count, eng), reads, writes)
        return ins

    def dma(self, eng, out, in_, reads=(), writes=(), key=None, **kw):
        if self.dead:
            return None
        e = self.engs[eng]
        self._wait(e, self._deps(reads, writes), is_dma=True)
        ds = self._dsem(key if key is not None else writes[0])
        ins = e.h.dma_start(out=out, in_=in_, **kw)
        ds[1] += 16
        ins.then_inc(ds[0], 16)
        self._commit((ds[0], ds[1], "dma"), reads, writes)
        return ins

    def finish(self, eng="sp"):
        deps = list(self.last_w.values())
        for v in self.readers.values():
            deps.extend(v)
        self._wait(self.engs[eng], deps, is_dma=True)


def _bucket(d):
    d = np.maximum(d, 0)
    nf = np.maximum(d, 1).astype(np.float32)
    large = 16 + (np.log(nf / np.float32(16)) / np.float32(math.log(128 / 16)) * np.float32(16)).astype(np.int32)
    large = np.minimum(large, 31)
    return np.where(d < 16, d, large)


def _oh_cols(dvals, allowed):
    n = len(dvals)
    oh = np.zeros((33, n), np.float32)
    b = _bucket(np.asarray(dvals))
    for i in range(n):
        if allowed[i]:
            oh[b[i], i] += 1.0
            oh[31, i] -= 1.0
        else:
            oh[32, i] = 1.0
    return oh


def _tables():
    j = np.arange(-127, 128)
    cols = []
    cols.append(_oh_cols(j, (j >= 0) & (j < 128)))
    cols.append(_oh_cols(128 + j, (128 + j >= 0) & (128 + j < 128)))
    cols.append(_oh_cols(j, j >= 0))
    cols.append(_oh_cols(128 + j, 128 + j >= 0))
    jm_ = np.arange(-15, 128)
    dm = 16 + jm_
    cols.append(_oh_cols(dm, np.ones_like(dm, bool)))
    cols.append(_oh_cols(dm, np.ones_like(dm, bool)))
    jq = np.arange(-15, 16)
    cols.append(_oh_cols(jq, jq >= 0))
    cols.append(_oh_cols(jq, jq >= 0))
    oh = np.concatenate(cols, 1)
    assert oh.shape[1] == NF
    qq = np.arange(128)[:, None]
    kk = np.arange(128)[None, :]
    cm = np.where(qq - kk >= 0, 0.0, -1e30).astype(np.float32)
    kq = np.zeros((128, 33), np.float32)
    for s in range(32):
        pos = 16 + 128 * s + np.arange(128)
        kq[:, s] = np.minimum(TOPK, pos + 1)
    kq[:, 32] = 1.0
    kq[:16, 32] = np.arange(16) + 1
    cmq = np.zeros((128, 16), np.float32)
    cmq[:16] = np.where(np.arange(16)[:, None] >= np.arange(16)[None, :], 0.0, -1e30)
    eye = np.eye(128, dtype=np.float32)
    anti = np.ascontiguousarray(eye[::-1])
    jm = np.zeros((128, 256), np.float32)
    jm[:, 0:128] = anti
    for i in range(16):
        jm[i, 128 + 15 - i] = 1.0
    jm[16, 128 + 16] = 1.0
    bis = np.tile((0.5 ** (np.arange(NBIS) + 1)).astype(np.float32)[None, :], (128, 1))
    return dict(oh=oh, cmask=cm, kq=kq, cmq=cmq, ident=eye, bis=bis, jm=jm)


def _pc(v, n):
    return np.ascontiguousarray(np.asarray(v, np.float32).reshape(n, 128).T)


def build(nlayer=DEPTH, dbg=False, stop=None, groups=None):
    nc = bass.Bass("TRN2", target_bir_lowering=False)
    groups = GROUPS if groups is None else groups

    def din(name, shape, dt=F32):
        return nc.dram_tensor(name, list(shape), dt, kind="ExternalInput").ap()

    def dout(name, shape, dt=F32):
        return nc.dram_tensor(name, list(shape), dt, kind="ExternalOutput").ap()

    hT = din("hT", [D, NT])
    ident_d = din("ident", [128, 128])
    g_all = din("g_all", [DEPTH + 1, 128, 16])
    w_in_d = din("w_in", [DEPTH, D, 8264])
    qg_d = din("qg", [DEPTH, 128, 4])
    kvg_d = din("kvg", [DEPTH, 128, 2])
    w_qb_d = din("w_qb", [DEPTH, 512, 1024])
    w_iq_d = din("w_iq", [DEPTH, 512, 512])
    w_uk_d = din("w_uk", [DEPTH, 16, 64, 256])
    w_uv_d = din("w_uv", [DEPTH, 16, 256, 64])
    sinks_d = din("sinks", [DEPTH, 16])
    w_pa_d = din("w_pa", [DEPTH, 1024, D])
    w_pb_d = din("w_pb", [DEPTH, 1024, D])
    w_out_d = din("w_out", [DEPTH, D, D])
    btab_d = din("btab", [32, 32])
    cmask_d = din("cmask", [128, 128])
    kq_d = din("kq", [128, 33])
    cmq_d = din("cmq", [128, 16])
    oh_d = din("oh", [33, NF])
    bis_d = din("bis", [128, NBIS])
    jm_d = din("jm", [128, 256])
    final_out = dout("out", [D, SEQ])
    fscr = nc.dram_tensor("fscr", [32, NF], BF16)
    sscr = nc.dram_tensor("sscr", [16, 16, 128], BF16)
    hbuf = [nc.dram_tensor("hbuf%d" % i, [D, NT], F32).ap() for i in range(2)]
    ksA = nc.dram_tensor("ksA", [128, NT], BF16).ap()
    ksV = nc.dram_tensor("ksV", [NT, 128], BF16).ap()
    dbg_out = dout("dbg", [128, 4112]) if dbg else None

    st = ExitStack()
    with st:
        S = Sync(nc, st)

        def sb(name, shape, dt):
            return st.enter_context(nc.sbuf_tensor("s_" + name, list(shape), dt))

        def ps(name, shape=(128, 512), dt=F32):
            return st.enter_context(nc.psum_tensor(name, list(shape), dt))

        hrot = sb("hrot", [128, 3, 272], F32)
        u = sb("u", [128, 16, 272], BF16)
        rstd = sb("rstd", [128, 272], F32)
        sq = sb("sq", [128, 2, 272], F32)
        wbuf = sb("wbuf", [128, 4, 4096], BF16)
        identf = sb("identf", [128, 128], F32)
        identb = sb("identb", [128, 128], BF16)
        onesf = sb("onesf", [128, 128], F32)
        onesb = sb("onesb", [128, 128], BF16)
        gall = sb("gall", [128, DEPTH + 1, 16], F32)
        kvg = sb("kvg", [128, DEPTH, 2], F32)
        qg = sb("qg", [128, DEPTH, 4], F32)
        kso = sb("kso", [128, 272], BF16)
        kso_tm = sb("kso_tm", [128, 128], BF16)
        ckv32 = sb("ckv32", [128, 2, 272], F32)
        vfm = sb("vfm", [128, 272], BF16)
        pj = sb("pj", [128, NPJ, 272], BF16)
        cq32 = sb("cq32", [128, 4, 272], F32)
        kfm = sb("kfm", [128, 3, L], BF16)
        ktm = sb("ktm", [128, 33, 256], BF16)
        swk = sb("swk", [128, 2, 2, 272], BF16)
        swv = sb("swv", [128, 2, 3, 256], BF16)
        isc = sb("isc", [128, L], F32)
        mneg = sb("mneg", [128, L], BF16)
        TA = sb("TA", [128, 2, 16, 128], BF16)
        TB = sb("TB", [128, 2, 16, 128], BF16)
        TmA0 = sb("TmA0", [128, 16, 128], BF16)
        TmB0 = sb("TmB0", [128, 16, 128], BF16)
        TmZ = sb("TmZ", [128, 16, 128], BF16)
        TqA = sb("TqA", [128, 16, 16], BF16)
        TqB = sb("TqB", [128, 16, 16], BF16)
        I4 = sb("I4", [128, 4, 128], BF16)
        jmb = sb("jmb", [128, 256], BF16)
        qlat = sb("qlat", [128, 2, 4, 128], BF16)
        olat = sb("olat", [128, 2, 4, 128], BF16)
        PT = sb("PT", [128, 3, 512], BF16)
        rtmp = sb("rtmp", [128, 2, 512], F32)
        rden = sb("rden", [128, 512], F32)
        wuk = sb("wuk", [128, 8, 256], BF16)
        wuv = sb("wuv", [128, 2, 16, 64], BF16)
        wq = sb("wq", [128, 3, 8], F32)
        kq = sb("kq", [128, 33], F32)
        cmask = sb("cmask", [128, 128], F32)
        cmq = sb("cmq", [128, 16], F32)
        bis = sb("bis", [128, NBIS], F32)
        btab = sb("btab", [128, 32], F32)
        fsb = sb("fsb", [128, NF], BF16)
        sk = sb("sk", [128, 16], F32)
        bsm = sb("bsm", [128, 16], F32)
        wh = sb("wh", [128, NBIS], F32)
        gsig = sb("gsig", [128, 2, 272], F32)
        mtmp = sb("mtmp", [128, 272], F32)
        mtmp2 = sb("mtmp2", [128, 272], F32)
        wid = sb("wid", [128, 16, 8], BF16)

        P = [ps("P%d" % i) for i in range(7)]
        PTR = ps("PTR", (128, 1024), BF16)

        def mm(out, lhsT, rhs, start, stop, reads, writes):
            return S.op("pe", lambda e: e.matmul(out, lhsT=lhsT, rhs=rhs, start=start, stop=stop),
                        reads=reads, writes=writes)

        def act(out, in_, func, reads, writes, scale=1.0, bias=0.0):
            return S.op("act", lambda e: e.activation(out=out, in_=in_, func=func, scale=scale, bias=bias),
                        reads=reads, writes=writes)

        def vcopy(eng, out, in_, reads, writes):
            return S.op(eng, lambda e: e.tensor_copy(out=out, in_=in_), reads=reads, writes=writes)

        def tt(out, in0, in1, op, reads, writes, eng="dve"):
            return S.op(eng, lambda e: e.tensor_tensor(out=out, in0=in0, in1=in1, op=op), reads=reads, writes=writes)

        def ts(out, in0, s1, op0, reads, writes, s2=None, op1=None, eng="dve", accum=None):
            def f(e):
                kw = {}
                if accum is not None:
                    kw["accum_out"] = accum
                if op1 is None:
                    return e.tensor_scalar(out=out, in0=in0, scalar1=s1, scalar2=None, op0=op0, **kw)
                return e.tensor_scalar(out=out, in0=in0, scalar1=s1, scalar2=s2, op0=op0, op1=op1, **kw)
            return S.op(eng, f, reads=reads, writes=writes)

        def stt(out, in0, scalar, in1, op0, op1, reads, writes):
            return S.op("dve", lambda e: e.scalar_tensor_tensor(out=out, in0=in0, scalar=scalar, in1=in1,
                                                                  op0=op0, op1=op1), reads=reads, writes=writes)

        wcount = [0]
        ucount = [0]
        wfirst = [True]
        wcache = {}

        def wload(src_ap, kc, mu):
            slot = wcount[0] % 4
            wcount[0] += 1
            uid = ucount[0]
            ucount[0] += 1
            n = kc * mu
            sname_ = "wbuf%d" % slot
            dst = wbuf[:, slot, 0:n].rearrange("p (k m) -> p k m", k=kc)
            if uid not in wcache:
                wcache[uid] = (nc.dram_tensor("wc%d" % uid, [128, n], BF16).ap(), n)
            cap, cn = wcache[uid]
            assert cn == n
            if wfirst[0]:
                S.dma("pool", dst, src_ap.rearrange("(k p) m -> p k m", p=128), writes=[sname_])
                S.dma("pool", cap[:, :], wbuf[:, slot, 0:n], reads=[sname_], writes=["wc%d" % uid], key="wcw%d" % slot)
            else:
                S.dma("pool", wbuf[:, slot, 0:n], cap[:, :], reads=["wc%d" % uid], writes=[sname_], key=sname_)
            return dst, sname_

        pcount = [0]

        pbset = [[5, 6]]

        def pbank():
            b_ = pbset[0][pcount[0] % len(pbset[0])]
            pcount[0] += 1
            return P[b_], "P%d" % b_

        def ckpt(name, dump=None, dreads=()):
            if stop == name:
                if dump is not None and dbg_out is not None:
                    ap, shape = dump
                    S.dma("pool", dbg_out[0:shape[0], 0:shape[1]], ap, reads=list(dreads), writes=["dbg_d"])
                S.dead = True

        S.dma("sp", identf[:], ident_d[:, :], writes=["identf"])
        vcopy("dve", identb[:], identf[:], ["identf"], ["identb"])
        S.op("dve", lambda e: e.memset(onesf[:], 1.0), writes=["onesf"])
        S.op("dve", lambda e: e.memset(onesb[:], 1.0), writes=["onesb"])
        S.dma("sp", gall[:], g_all.rearrange("l p c -> p l c"), writes=["gall"])
        S.dma("sp", kvg[:], kvg_d.rearrange("l p c -> p l c"), writes=["kvg"])
        S.dma("sp", qg[:], qg_d.rearrange("l p c -> p l c"), writes=["qg"])
        S.dma("sp", kq[:], kq_d[:, :], writes=["kq"])
        S.dma("sp", cmask[:], cmask_d[:, :], writes=["cmask"])
        S.dma("sp", cmq[:], cmq_d[:, :], writes=["cmask"])
        S.dma("sp", bis[:], bis_d[:, :], writes=["bis"])
        S.dma("pool", jmb[:], jm_d[:, :], writes=["jmb"])
        for i in range(4):
            vcopy("dve", I4[:, i, :], identb[:], ["identb"], ["I4"])
        S.op("dve", lambda e: e.memset(swv[:], 1.0), writes=["swv0", "swv1"])
        for b_ in range(2):
            for g_ in range(2):
                S.op("dve", lambda e, b_=b_, g_=g_: e.memset(swv[0:32, b_, 0, g_ * 128:g_ * 128 + 64], 0.0),
                     writes=["swv%d" % b_])
        S.dma("sp", btab[0:32, :], btab_d[:, :], writes=["btab"])
        S.op("dve", lambda e: e.memset(btab[32:33, :], NEGB / 8.0), writes=["btab32"])
        S.dma("sp", isc[0:33, 0:NF], oh_d[:, :], writes=["isc"])
        for c0 in range(0, NF, 512):
            w_ = min(512, NF - c0)
            pb, pn = pbank()
            mm(pb[0:32, 0:w_], btab[0:33, :], isc[0:33, c0:c0 + w_], True, True, ["btab", "btab32", "isc"], [pn])
            ts(fsb[0:32, c0:c0 + w_], pb[0:32, 0:w_], 8.0, ALU.mult, [pn], ["fsb"])
        S.dma("sp", fscr.ap()[:, :], fsb[0:32, :], reads=["fsb"], writes=["fscr"])

        def toep(dst, nk, nq_, row0, base):
            src = bass.AP(tensor=fscr, offset=row0 * NF + base, ap=[[1, nk], [NF, 16], [1, nq_]])
            S.dma("sp", dst, src, reads=["fscr"], writes=["T"])

        for dl in range(2):
            toep(TA[:, dl, :, :], 128, 128, 0, FB_A0 + 255 * dl)
            toep(TB[:, dl, :, :], 128, 128, 16, FB_B0 + 255 * dl)
        S.op("dve", lambda e: e.memset(TmZ[0:32, :, :], 0.0), writes=["T"])
        S.op("dve", lambda e: e.memset(TmA0[0:32, :, :], 0.0), writes=["T"])
        S.op("dve", lambda e: e.memset(TqA[0:32, :, :], 0.0), writes=["T"])
        toep(TmA0[0:16, :, :], 16, 128, 0, FB_MA)
        toep(TmB0[0:16, :, :], 16, 128, 16, FB_MB)
        toep(TqA[0:16, :, :], 16, 16, 0, FB_QA)
        toep(TqB[0:16, :, :], 16, 16, 16, FB_QB)
        ckpt("consts")

        def layer_consts(l):
            S.dma("pool", wuk[:], w_uk_d[l].rearrange("(hp two) d r -> (two d) hp r", two=2), writes=["wuk"])
            for c_ in range(2):
                S.dma("pool", wuv[:, c_, :, :], w_uv_d[l][:, c_ * 128:(c_ + 1) * 128, :].rearrange("h p d -> p h d"),
                      writes=["wuv"])
            S.dma("sp", sk[0:1, :], sinks_d[l:l + 1, :], writes=["sk"])
            S.dma("sp", bsm[0:1, 0:16], btab_d[31:32, 0:16], writes=["bsm"])
            tt(sk[0:1, :], sk[0:1, :], bsm[0:1, 0:16], ALU.subtract, ["sk", "bsm"], ["sk"])
            ts(sk[0:1, :], sk[0:1, :], 8.0, ALU.mult, ["sk"], ["sk"])
            sk2 = PT[0:1, 0, 0:512]
            for hh in range(16):
                ts(PT[0:1, hh % 3, 0:128], onesf[0:1, :], sk[0:1, hh:hh + 1], ALU.mult, ["sk", "onesf"], ["PT%d" % (hh % 3)])
                S.dma("sp", sscr.ap()[0:1, hh, :], PT[0:1, hh % 3, 0:128], reads=["PT%d" % (hh % 3)], writes=["sscr"])
            S.dma("sp", TmZ[16:17, :, :], sscr.ap()[0:1, :, :], reads=["sscr"], writes=["T"])
            S.dma("sp", TmA0[16:17, :, :], sscr.ap()[0:1, :, :], reads=["sscr"], writes=["T"])
            S.dma("sp", TqA[16:17, :, :], sscr.ap()[0:1, :, 0:16], reads=["sscr"], writes=["T"])

        def rmsnorm_group(src_dram, c0, ncol, gvec, gname, sname):
            pb, pn = pbank()
            for c in range(16):
                slot = c % 3
                S.dma("sp", hrot[:, slot, 0:ncol], src_dram[c * 128:(c + 1) * 128, c0:c0 + ncol],
                      reads=[sname], writes=["hrot%d" % slot])
                act(sq[:, c % 2, 0:ncol], hrot[:, slot, 0:ncol], AF.Square, ["hrot%d" % slot], ["sq%d" % (c % 2)])
                mm(pb[:, 0:ncol], onesf[:], sq[:, c % 2, 0:ncol], c == 0, c == 15, ["onesf", "sq%d" % (c % 2)], [pn])
            act(rstd[:, 0:ncol], pb[:, 0:ncol], AF.Sqrt, [pn], ["rstd"], scale=1.0 / D, bias=1e-6)
            S.op("dve", lambda e: e.reciprocal(out=rstd[:, 0:ncol], in_=rstd[:, 0:ncol]), reads=["rstd"], writes=["rstd"])

        def subnorm(x32, nch, ncol, nfeat, gvec_fn, gname, out_fn, rname):
            pb, pn = pbank()
            for c in range(nch):
                act(sq[:, c % 2, 0:ncol], x32[:, c, 0:ncol], AF.Square, [rname], ["sq%d" % (c % 2)])
                mm(pb[:, 0:ncol], onesf[:], sq[:, c % 2, 0:ncol], c == 0, c == nch - 1, ["onesf", "sq%d" % (c % 2)], [pn])
            act(mtmp2[:, 0:ncol], pb[:, 0:ncol], AF.Sqrt, [pn], ["mtmp2"], scale=1.0 / nfeat, bias=1e-6)
            S.op("dve", lambda e: e.reciprocal(out=mtmp2[:, 0:ncol], in_=mtmp2[:, 0:ncol]), reads=["mtmp2"], writes=["mtmp2"])
            for c in range(nch):
                dst, dname = out_fn(c)
                stt(dst, x32[:, c, 0:ncol], gvec_fn(c), mtmp2[:, 0:ncol], ALU.mult, ALU.mult,
                    [rname, gname, "mtmp2"], [dname])

        def linear(w_ap, kc, m_total, rhs_fn, rhs_reads, ncol, evac, mu=256):
            mu = min(mu, m_total, 4096 // kc)
            for m0 in range(0, m_total, mu):
                mw = min(mu, m_total - m0)
                wt, wn = wload(w_ap[:, m0:m0 + mw], kc, mw)
                for j in range(0, mw, 128):
                    mwid = min(128, mw - j)
                    pb, pn = pbank()
                    for k in range(kc):
                        mm(pb[0:mwid, 0:ncol], wt[:, k, j:j + mwid], rhs_fn(k), k == 0, k == kc - 1, [wn] + rhs_reads, [pn])
                    evac((m0 + j) // 128, pb, pn, mwid)

        def kside_group(l, c0, ncol, slots):
            w_in = w_in_d[l]

            def ev_to(dst_fn):
                def ev(mc, pb, pn, mwid):
                    dst, dn = dst_fn(mc)
                    act(dst, pb[0:mwid, 0:ncol], AF.Copy, [pn], [dn])
                return ev
            ur = lambda k: u[:, k, 0:ncol]
            linear(w_in[:, O_KA:O_KA + 128], 16, 128, ur, ["u"], ncol, ev_to(lambda mc: (kso[:, 0:ncol], "kso")), mu=128)
            S.dma("sp", ksA[:, c0:c0 + ncol], kso[:, 0:ncol], reads=["kso"], writes=["ksA_d"])
            linear(w_in[:, O_VA:O_VA + 128], 16, 128, ur, ["u"], ncol, ev_to(lambda mc: (vfm[:, 0:ncol], "vfm")), mu=128)
            linear(w_in[:, O_CKV:O_CKV + 256], 16, 256, ur, ["u"], ncol,
                   ev_to(lambda mc: (ckv32[:, mc, 0:ncol], "ckv32")))
            for half in range(2):
                pass
            wt, wn = wload(w_in[:, O_KI:O_KI + 64], 16, 64)
            pb, pn = pbank()
            for half in range(2):
                for k in range(16):
                    mm(pb[half * 64:half * 64 + 64, 0:ncol], wt[:, k, :], u[:, k, 0:ncol], k == 0, k == 15, [wn, "u"], [pn])
            act(kfm[:, 2, c0:c0 + ncol], pb[:, 0:ncol], AF.Copy, [pn], ["kfm"])
            subnorm(ckv32, 2, ncol, 256, lambda c: kvg[:, l, c:c + 1], "kvg",
                    lambda c: (kfm[:, c, c0:c0 + ncol], "kfm"), "ckv32")
            for (s, loc, nq) in slots:
                tix = 0 if s == 32 else 1 + s
                srcs = [vfm[:, loc:loc + nq], kfm[:, 0, c0 + loc:c0 + loc + nq], kfm[:, 1, c0 + loc:c0 + loc + nq]]
                for i, src in enumerate(srcs):
                    S.op("pe", lambda e, src=src, i=i: e.transpose(PTR[0:nq, i * 128:(i + 1) * 128], src, identb[:]),
                         reads=["vfm", "kfm", "identb"], writes=["PTR"])
                vcopy("dve", kso_tm[0:nq, :], PTR[0:nq, 0:128], ["PTR"], ["kso_tm"])
                vcopy("dve", ktm[0:nq, tix, :], PTR[0:nq, 128:384], ["PTR"], ["ktm"])
                S.dma("sp", ksV[c0 + loc:c0 + loc + nq, :], kso_tm[0:nq, :], reads=["kso_tm"], writes=["ksV_d"])

        def swa_qblock(s, qc0, nq):
            buf = s % 2
            sk_, sv_ = "swk%d" % buf, "swv%d" % buf
            pos0 = 16 + 128 * s if s < 32 else 0
            pieces = [(0, 0, 16)]
            if s < 32:
                if s >= 1:
                    pieces.append((16, pos0 - 128, 256))
                else:
                    pieces.append((144, pos0, 128))
            for (dcol, scol, w_) in pieces:
                S.dma("sp", swk[:, buf, 0, dcol:dcol + w_], ksA[:, scol:scol + w_], reads=["ksA_d"], writes=[sk_])
                S.dma("sp", swk[0:64, buf, 1, dcol:dcol + w_], ksA[64:128, scol:scol + w_], reads=["ksA_d"], writes=[sk_])
                S.dma("sp", swk[64:128, buf, 1, dcol:dcol + w_], ksA[0:64, scol:scol + w_], reads=["ksA_d"], writes=[sk_])
            vt = [(0, 0, 16)]
            if s < 32:
                if s >= 1:
                    vt.append((1, pos0 - 128, 128))
                vt.append((2, pos0, 128))
            for (t_, row0, n_) in vt:
                S.dma("sp", swv[0:n_, buf, t_, :].rearrange("p (g c) -> p g c", g=2)[:, :, 0:64],
                      ksV[row0:row0 + n_, :].rearrange("p (g c) -> p g c", g=2), reads=["ksV_d"], writes=[sv_])
            if s == 32:
                tiles = [(0, 17, 16, TqA, None)]
            else:
                mt = TmA0 if s == 0 else TmZ
                tiles = [(0, 17, 16, mt, None)]
                if s >= 1:
                    tiles.append((1, 128, 128, TA, 1))
                tiles.append((2, 128, 128, TA, 0))
            for gi in range(4):
                g = gi // 2
                OA, DN = P[2], P[3]
                def emit_S(ti):
                    tix, kb, kk, Tt, dl = tiles[ti]
                    kcol = [0, 16, 144][tix]
                    pt = PT[:, ti % 3, :]
                    ptn = "PT%d" % (ti % 3)
                    jl = jmb[0:kb, 128:128 + kb] if kb == 17 else jmb[0:128, 0:128]
                    for par in range(2):
                        Sb = P[ti % 2] if par == 0 else P[5 + ti % 2]
                        sn = ("P%d" % (ti % 2)) if par == 0 else ("P%d" % (5 + ti % 2))
                        hsel = slice(4 * gi + par, 4 * gi + 4, 2)
                        rhs = Tt[0:kb, hsel, 0:nq] if dl is None else Tt[0:kb, dl, hsel, 0:nq]
                        mm(Sb[0:kb, 0:2 * nq].rearrange("p (a q) -> p a q", a=2), jl, rhs, True, False, ["jmb", "T"], [sn])
                        for a in range(2):
                            h = 4 * gi + 2 * a + par
                            pb_ = par * 64
                            v = 0 if (g == par) else 1
                            mm(Sb[0:kk, a * nq:(a + 1) * nq], swk[pb_:pb_ + 64, buf, v, kcol:kcol + kk],
                               pj[pb_:pb_ + 64, C_QA + h // 2, qc0:qc0 + nq], False, a == 1, [sk_, "pjqa"], [sn])
                        act(pt[0:kb, par * 2 * nq:(par + 1) * 2 * nq], Sb[0:kb, 0:2 * nq], AF.Exp, [sn], [ptn], scale=0.125)

                def emit_PV(ti):
                    tix, kb, kk, Tt, dl = tiles[ti]
                    pt = PT[:, ti % 3, :]
                    ptn = "PT%d" % (ti % 3)
                    first, last = ti == 0, ti == len(tiles) - 1
                    for par in range(2):
                        ptv = pt[0:kb, par * 2 * nq:(par + 1) * 2 * nq]
                        mm(OA[par * 64:par * 64 + 64, 0:2 * nq], swv[0:kb, buf, tix, g * 128:g * 128 + 64], ptv, first, last,
                           [sv_, ptn], ["P2"])
                        mm(DN[par * 64:par * 64 + 64, 0:2 * nq], swv[0:kb, buf, tix, g * 128 + 64:g * 128 + 128], ptv, first, last,
                           [sv_, ptn], ["P3"])

                emit_S(0)
                for ti in range(len(tiles)):
                    if ti + 1 < len(tiles):
                        emit_S(ti + 1)
                    emit_PV(ti)
                S.op("dve", lambda e: e.reciprocal(out=rden[:, 0:2 * nq], in_=DN[:, 0:2 * nq]), reads=["P3"], writes=["rden"])
                tt(rden[:, 0:2 * nq], rden[:, 0:2 * nq], OA[:, 0:2 * nq], ALU.mult, ["rden", "P2"], ["rden"])
                for a in range(2):
                    dst = pj[:, C_ZA + 2 * gi + a, qc0:qc0 + nq]
                    tt(dst, dst, rden[:, a * nq:(a + 1) * nq], ALU.mult, ["pjza", "rden"], ["pjza"])

        def dsa_qblock(s, qc0, nq, wslot):
            ntr = 0 if s == 32 else s + 1
            K = 16 + 128 * ntr
            for k0 in range(0, K, 512):
                kw = min(512, K - k0)
                for hi in range(8):
                    pb_ = (hi % 2) * 64
                    pb, pn = pbank()
                    mm(pb[0:nq, 0:kw], pj[pb_:pb_ + 64, C_QI + hi // 2, qc0:qc0 + nq], kfm[pb_:pb_ + 64, 2, k0:k0 + kw],
                       True, True, ["pjqi", "kfm"], [pn])
                    rt = rtmp[:, hi % 2, :]
                    rn = "rtmp%d" % (hi % 2)
                    act(rt[0:nq, 0:kw], pb[0:nq, 0:kw], AF.Relu, [pn], [rn])
                    if hi == 0:
                        ts(isc[0:nq, k0:k0 + kw], rt[0:nq, 0:kw], wq[0:nq, wslot, 0:1], ALU.mult, [rn, "wq"], ["isc"])
                    else:
                        stt(isc[0:nq, k0:k0 + kw], rt[0:nq, 0:kw], wq[0:nq, wslot, hi:hi + 1], isc[0:nq, k0:k0 + kw],
                            ALU.mult, ALU.add, [rn, "wq", "isc"], ["isc"])
            if s == 32:
                n0, cm = 0, cmq[0:nq, 0:16]
            else:
                n0, cm = 16 + 128 * s, cmask[0:nq, 0:128]
            near = isc[0:nq, n0:K]
            nw = K - n0
            lo, hi_, mid, cnt, stp, w0, m1, m2 = [bsm[0:nq, i:i + 1] for i in range(8)]
            tt(rtmp[0:nq, 0, 0:nw], near, cm, ALU.subtract, ["isc", "cmask"], ["rtmp0"])
            S.op("dve", lambda e: e.tensor_reduce(out=m1, in_=rtmp[0:nq, 0, 0:nw], op=ALU.min, axis=AXX),
                 reads=["rtmp0"], writes=["bsm"])
            if n0 > 0:
                S.op("dve", lambda e: e.tensor_reduce(out=m2, in_=isc[0:nq, 0:n0], op=ALU.min, axis=AXX),
                     reads=["isc"], writes=["bsm"])
                tt(m1, m1, m2, ALU.min, ["bsm"], ["bsm"])
            tt(near, near, cm, ALU.add, ["isc", "cmask"], ["isc"])
            S.op("dve", lambda e: e.tensor_reduce(out=hi_, in_=isc[0:nq, 0:K], op=ALU.max, axis=AXX),
                 reads=["isc"], writes=["bsm"])
            vcopy("dve", lo, m1, ["bsm"], ["bsm"])
            stt(w0, hi_, 1e-3, lo, ALU.add, ALU.subtract, ["bsm"], ["bsm"])
            ts(wh[0:nq, :], bis[0:nq, :], w0, ALU.mult, ["bis", "bsm"], ["wh"])
            kqv = kq[0:nq, s:s + 1]
            for it in range(NBIS):
                tt(mid, lo, wh[0:nq, it:it + 1], ALU.add, ["bsm", "wh"], ["bsm"])
                ts(mneg[0:nq, 0:K], isc[0:nq, 0:K], mid, ALU.is_ge, ["isc", "bsm"], ["mneg", "bsm"],
                   s2=0.0, op1=ALU.add, accum=cnt)
                stt(stp, cnt, kqv, wh[0:nq, it:it + 1], ALU.is_ge, ALU.mult, ["bsm", "kq", "wh"], ["bsm"])
                tt(lo, lo, stp, ALU.add, ["bsm"], ["bsm"])
            ts(mneg[0:nq, 0:K], isc[0:nq, 0:K], lo, ALU.is_lt, ["isc", "bsm"], ["mneg"], s2=NEGB, op1=ALU.mult)
            ktiles = [(0, 0, 16)] + [(1 + t, 16 + 128 * t, 128) for t in range(ntr)]
            O0, O1, DN = P[2], P[3], P[4]
            nt_ = len(ktiles)

            def emit_qlat(gi):
                for c in range(2):
                    for par in range(2):
                        pb, pn = pbank()
                        pb_ = par * 64
                        for a in range(2):
                            h = 4 * gi + 2 * a + par
                            mm(pb[:, a * nq:(a + 1) * nq], wuk[pb_:pb_ + 64, h // 2, c * 128:(c + 1) * 128],
                               pj[pb_:pb_ + 64, C_QB + h // 2, qc0:qc0 + nq], True, True, ["wuk", "pjqb"], [pn])
                        for a in range(2):
                            act(qlat[:, c, 2 * a + par, 0:nq], pb[:, a * nq:(a + 1) * nq], AF.Copy, [pn], ["qlat"])

            def emit_S(gi, ti):
                tix, kcol, kk = ktiles[ti]
                Sb = P[ti % 2]
                sn = "P%d" % (ti % 2)
                Sv = Sb[0:kk, 0:4 * nq].rearrange("p (h q) -> p h q", h=4)
                extra = []
                if s < 32 and tix >= 1:
                    dl = s - (tix - 1)
                    if dl in (0, 1):
                        extra.append((jmb[0:128, 0:128], TB[:, dl, 4 * gi:4 * gi + 4, 0:nq]))
                if tix == 0 and s == 0:
                    extra.append((jmb[0:16, 128:144], TmB0[0:16, 4 * gi:4 * gi + 4, 0:nq]))
                if s == 32:
                    extra.append((jmb[0:16, 128:144], TqB[0:16, 4 * gi:4 * gi + 4, 0:nq]))
                for c in range(2):
                    mm(Sv, kfm[:, c, kcol:kcol + kk], qlat[:, c, :, 0:nq], c == 0, False, ["kfm", "qlat"], [sn])
                mm(Sv, mneg[0:nq, kcol:kcol + kk], I4[0:nq, :, 0:nq], False, len(extra) == 0, ["mneg", "I4"], [sn])
                for ei, (lt, rh) in enumerate(extra):
                    mm(Sv, lt, rh, False, ei == len(extra) - 1, ["jmb", "T"], [sn])
                pt = PT[:, ti % 3, :]
                act(pt[0:kk, 0:4 * nq], Sb[0:kk, 0:4 * nq], AF.Exp, [sn], ["PT%d" % (ti % 3)], scale=0.125)

            def emit_PV(ti):
                tix, kcol, kk = ktiles[ti]
                pt = PT[:, ti % 3, :]
                ptn = "PT%d" % (ti % 3)
                first, last = ti == 0, ti == nt_ - 1
                mm(O0[:, 0:4 * nq], ktm[0:kk, tix, 0:128], pt[0:kk, 0:4 * nq], first, last, ["ktm", ptn], ["P2"])
                mm(O1[:, 0:4 * nq], ktm[0:kk, tix, 128:256], pt[0:kk, 0:4 * nq], first, last, ["ktm", ptn], ["P3"])
                mm(DN[:, 0:4 * nq], onesb[0:kk, :], pt[0:kk, 0:4 * nq], first, last, ["onesb", ptn], ["P4"])

            def emit_epi(gi):
                S.op("dve", lambda e: e.reciprocal(out=rden[:, 0:4 * nq], in_=DN[:, 0:4 * nq]), reads=["P4"], writes=["rden"])
                tt(olat[:, 0, :, 0:nq], O0[:, 0:4 * nq].rearrange("p (h q) -> p h q", h=4),
                   rden[:, 0:4 * nq].rearrange("p (h q) -> p h q", h=4), ALU.mult, ["P2", "rden"], ["olat"])
                tt(olat[:, 1, :, 0:nq], O1[:, 0:4 * nq].rearrange("p (h q) -> p h q", h=4),
                   rden[:, 0:4 * nq].rearrange("p (h q) -> p h q", h=4), ALU.mult, ["P3", "rden"], ["olat"])

            def emit_out(gi):
                pb, pn = pbank()
                for hh in range(4):
                    h = 4 * gi + hh
                    a, par = hh // 2, hh % 2
                    for c in range(2):
                        mm(pb[par * 64:par * 64 + 64, a * nq:(a + 1) * nq], wuv[:, c, h, :], olat[:, c, hh, 0:nq],
                           c == 0, c == 1, ["wuv", "olat"], [pn])
                for a in range(2):
                    dst = pj[:, C_ZB + 2 * gi + a, qc0:qc0 + nq]
                    tt(dst, dst, pb[:, a * nq:(a + 1) * nq], ALU.mult, ["pjzb", pn], ["pjzb"])

            emit_qlat(0)
            emit_S(0, 0)
            for gi in range(4):
                for ti in range(nt_):
                    if ti + 1 < nt_:
                        emit_S(gi, ti + 1)
                    emit_PV(ti)
                emit_epi(gi)
                if gi + 1 < 4:
                    emit_qlat(gi + 1)
                    emit_S(gi + 1, 0)
                emit_out(gi)

        def layer_group(l, hsrc, sname, hdst, dname, c0, ncol, slots):
            w_in = w_in_d[l]
            pbset[0] = [0, 1, 2, 3, 4, 5, 6]
            rmsnorm_group(hsrc, c0, ncol, None, None, sname)
            for c in range(16):
                slot = c % 3
                S.dma("sp", hrot[:, slot, 0:ncol], hsrc[c * 128:(c + 1) * 128, c0:c0 + ncol],
                      reads=[sname], writes=["hrot%d" % slot])
                stt(u[:, c, 0:ncol], hrot[:, slot, 0:ncol], gall[:, l, c:c + 1], rstd[:, 0:ncol], ALU.mult, ALU.mult,
                    ["hrot%d" % slot, "gall", "rstd"], ["u"])
            kside_group(l, c0, ncol, slots)

            def ev_copy(chunk0, name):
                def ev(mc, pb, pn, mwid):
                    act(pj[0:mwid, chunk0 + mc, 0:ncol], pb[0:mwid, 0:ncol], AF.Copy, [pn], [name])
                return ev

            def ev_silu(chunk0, name):
                def ev(mc, pb, pn, mwid):
                    act(pj[0:mwid, chunk0 + mc, 0:ncol], pb[0:mwid, 0:ncol], AF.Silu, [pn], [name])
                return ev

            def ev_cq(mc, pb, pn, mwid):
                vcopy("dve", cq32[:, mc, 0:ncol], pb[:, 0:ncol], [pn], ["cq32"])

            ur = lambda k: u[:, k, 0:ncol]
            linear(w_in[:, O_QA:O_QA + 1024], 16, 1024, ur, ["u"], ncol, ev_copy(C_QA, "pjqa"))
            linear(w_in[:, O_ZA:O_ZA + 1024], 16, 1024, ur, ["u"], ncol, ev_silu(C_ZA, "pjza"))
            linear(w_in[:, O_ZB:O_ZB + 1024], 16, 1024, ur, ["u"], ncol, ev_silu(C_ZB, "pjzb"))
            linear(w_in[:, O_CQ:O_CQ + 512], 16, 512, ur, ["u"], ncol, ev_cq)
            S.dma("pool", wid[:], w_in[:, O_WI:O_WI + 8].rearrange("(k p) m -> p k m", p=128), writes=["wid"])
            for ti, (s, loc, nq) in enumerate(slots):
                pb, pn = pbank()
                for k in range(16):
                    mm(pb[0:nq, 0:8], u[:, k, loc:loc + nq], wid[:, k, :], k == 0, k == 15, ["u", "wid"], [pn])
                ts(wq[0:nq, ti, :], pb[0:nq, 0:8], (8.0 ** -0.5) * (64.0 ** -0.5), ALU.mult, [pn], ["wq"])
            subnorm(cq32, 4, ncol, 512, lambda c: qg[:, l, c:c + 1], "qg", lambda c: (pj[:, C_CQN + c, 0:ncol], "pjcqn"), "cq32")
            cr = lambda k: pj[:, C_CQN + k, 0:ncol]
            linear(w_qb_d[l], 4, 1024, cr, ["pjcqn"], ncol, ev_copy(C_QB, "pjqb"), mu=1024)
            linear(w_iq_d[l], 4, 512, cr, ["pjcqn"], ncol, ev_copy(C_QI, "pjqi"), mu=512)
            ckpt("B%d" % l)
            pbset[0] = [5, 6]
            for ti, (s, loc, nq) in enumerate(slots):
                swa_qblock(s, loc, nq)
                dsa_qblock(s, loc, nq, ti)
            ckpt("C%d" % l)

            def mgc(m):
                return C_QA + m if m < 8 else C_QB + (m - 8)

            pbset[0] = [0, 1, 4]
            for m in range(16):
                ia, ib = (5, 6) if m % 2 == 0 else (2, 3)
                pa, pb2 = P[ia], P[ib]
                na, nb = "P%d" % ia, "P%d" % ib
                wa, wan = wload(w_pa_d[l][:, m * 128:(m + 1) * 128], 8, 128)
                for k in range(8):
                    mm(pa[:, 0:ncol], wa[:, k, :], pj[:, C_ZA + k, 0:ncol], k == 0, k == 7, [wan, "pjza"], [na])
                vcopy("dve", gsig[:, 0, 0:ncol], pa[:, 0:ncol], [na], ["gsig0"])
                wb_, wbn = wload(w_pb_d[l][:, m * 128:(m + 1) * 128], 8, 128)
                for k in range(8):
                    mm(pb2[:, 0:ncol], wb_[:, k, :], pj[:, C_ZB + k, 0:ncol], k == 0, k == 7, [wbn, "pjzb"], [nb])
                vcopy("dve", gsig[:, 1, 0:ncol], pb2[:, 0:ncol], [nb], ["gsig1"])
                wg, wgn = wload(w_in[:, O_GA + m * 128:O_GA + (m + 1) * 128], 16, 128)
                for k in range(16):
                    mm(pa[:, 0:ncol], wg[:, k, :], u[:, k, 0:ncol], k == 0, k == 15, [wgn, "u"], [na])
                act(mtmp[:, 0:ncol], pa[:, 0:ncol], AF.Sigmoid, [na], ["mtmp"])
                tt(gsig[:, 0, 0:ncol], gsig[:, 0, 0:ncol], mtmp[:, 0:ncol], ALU.mult, ["gsig0", "mtmp"], ["gsig0"])
                wg, wgn = wload(w_in[:, O_GB + m * 128:O_GB + (m + 1) * 128], 16, 128)
                for k in range(16):
                    mm(pb2[:, 0:ncol], wg[:, k, :], u[:, k, 0:ncol], k == 0, k == 15, [wgn, "u"], [nb])
                act(mtmp[:, 0:ncol], pb2[:, 0:ncol], AF.Sigmoid, [nb], ["mtmp"])
                tt(gsig[:, 1, 0:ncol], gsig[:, 1, 0:ncol], mtmp[:, 0:ncol], ALU.mult, ["gsig1", "mtmp"], ["gsig1"])
                tt(pj[:, mgc(m), 0:ncol], gsig[:, 0, 0:ncol], gsig[:, 1, 0:ncol], ALU.add, ["gsig0", "gsig1"], ["pjqa", "pjqb"])

            def ev_res(mc, pb, pn, mwid):
                slot = mc % 3
                S.dma("sp", hrot[:, slot, 0:ncol], hsrc[mc * 128:(mc + 1) * 128, c0:c0 + ncol],
                      reads=[sname], writes=["hrot%d" % slot])
                tt(hrot[:, slot, 0:ncol], hrot[:, slot, 0:ncol], pb[:, 0:ncol], ALU.add, ["hrot%d" % slot, pn], ["hrot%d" % slot])
                S.dma("sp", hdst[mc * 128:(mc + 1) * 128, c0:c0 + ncol], hrot[:, slot, 0:ncol],
                      reads=["hrot%d" % slot], writes=[dname])
            pbset[0] = [0, 1, 2, 3, 4, 5, 6]
            linear(w_out_d[l], 16, D, lambda k: pj[:, mgc(k), 0:ncol], ["pjqa", "pjqb"], ncol, ev_res)
            pbset[0] = [5, 6]

        def final_group(hsrc, sname, c0, ncol):
            lo_ = max(c0, 16)
            n_ = c0 + ncol - lo_
            rmsnorm_group(hsrc, lo_, n_, None, None, sname)
            for c in range(16):
                slot = c % 3
                S.dma("sp", hrot[:, slot, 0:n_], hsrc[c * 128:(c + 1) * 128, lo_:lo_ + n_], reads=[sname], writes=["hrot%d" % slot])
                stt(hrot[:, slot, 0:n_], hrot[:, slot, 0:n_], gall[:, DEPTH, c:c + 1], rstd[:, 0:n_], ALU.mult, ALU.mult,
                    ["hrot%d" % slot, "gall", "rstd"], ["hrot%d" % slot])
                S.dma("sp", final_out[c * 128:(c + 1) * 128, lo_ - 16:lo_ - 16 + n_], hrot[:, slot, 0:n_],
                      reads=["hrot%d" % slot], writes=["final"])

        for l in range(nlayer):
            layer_consts(l)
            hsrc, sname = (hT, "hT_d") if l == 0 else (hbuf[l % 2], "hbuf%d" % (l % 2))
            hdst, dname = hbuf[(l + 1) % 2], "hbuf%d" % ((l + 1) % 2)
            for gi_, (c0, ncol, slots) in enumerate(groups):
                ucount[0] = 0
                wfirst[0] = (gi_ == 0)
                layer_group(l, hsrc, sname, hdst, dname, c0, ncol, slots)
        hsrc, sname = hbuf[nlayer % 2], "hbuf%d" % (nlayer % 2)
        for (c0, ncol, slots) in groups:
            final_group(hsrc, sname, c0, ncol)
        S.finish("sp")
        build.n_ops = {k: v.count for k, v in S.engs.items()}
    return nc


_NC = {}


def kernel(x, meta_tokens, bias_table, norm_g, w_in, q_norm_g, kv_norm_g, w_qb, w_iq,
           w_uk, w_uv, sinks, w_proj_a, w_proj_b, w_out, final_g):
    f = lambda a: np.ascontiguousarray(np.asarray(a, np.float32))
    x, meta_tokens, bias_table, norm_g, w_in = f(x), f(meta_tokens), f(bias_table), f(norm_g), f(w_in)
    q_norm_g, kv_norm_g, w_qb, w_iq, w_uk, w_uv = f(q_norm_g), f(kv_norm_g), f(w_qb), f(w_iq), f(w_uk), f(w_uv)
    sinks, w_proj_a, w_proj_b, w_out, final_g = f(sinks), f(w_proj_a), f(w_proj_b), f(w_out), f(final_g)
    cores = list(range(8))
    t = _tables()
    hTs = [np.ascontiguousarray(np.concatenate([meta_tokens, x[b]], 0).T) for b in range(2)]
    g_all = np.stack([_pc(norm_g[l], 16) for l in range(DEPTH)] + [_pc(final_g, 16)], 0)
    shared = dict(ident=t["ident"], g_all=g_all, w_in=w_in,
                  qg=np.stack([_pc(q_norm_g[l], 4) for l in range(DEPTH)], 0),
                  kvg=np.stack([_pc(kv_norm_g[l], 2) for l in range(DEPTH)], 0),
                  w_qb=w_qb, w_iq=w_iq, w_uk=w_uk, w_uv=w_uv, sinks=sinks, w_pa=w_proj_a, w_pb=w_proj_b,
                  w_out=w_out, btab=bias_table, cmask=t["cmask"], kq=t["kq"], cmq=t["cmq"], oh=t["oh"],
                  bis=t["bis"], jm=t["jm"])
    if "nc" not in _NC:
        _NC["nc"] = build()
    maps = []
    for c in cores:
        m = dict(shared)
        m["hT"] = hTs[c // 4]
        maps.append(m)
    res = run_bass_kernel_spmd(_NC["nc"], maps, core_ids=cores).results
    out = np.stack([np.ascontiguousarray(np.asarray(res[4 * b]["out"]).T) for b in range(2)], 0)
    return out.astype(np.float32)
```
